# Optimizing a Trainium2 kernel written in Bass

```python
import math
import jax, jax.numpy as jnp
from jax import lax
import numpy as np

D_MODEL = 2048
BATCH = 16
SEQ = 2048
DEPTH = 4
DEC_BATCH = 8
DEC_SEQ = 2048
PAST_LEN = 128

HEAD_DIM = 128
ATT_WIDTH = 3 * D_MODEL // 4
POOL_WIDTH = D_MODEL - ATT_WIDTH
N_HEADS = ATT_WIDTH // HEAD_DIM
DILATED_GROUPS = ((128, 1), (512, 4), (2048, 16))
N_GROUPS = len(DILATED_GROUPS)
QKV_WIDTH = N_GROUPS * 3 * ATT_WIDTH
IN_WIDTH = QKV_WIDTH + POOL_WIDTH
POOL_WINDOWS = (2, 4, 8, 16)
POOL_GROUP = POOL_WIDTH // len(POOL_WINDOWS)
D_FF = 5632
N_BUCKETS = 32
MAX_DISTANCE = 1024
EPS = 1e-6
NEG_INF = -1e30

kernel_name = "hybrid_dilated_attn_pool_macaron_encoder"


def rmsnorm(x, g):
    xf = x.astype(jnp.float32)
    y = xf * lax.rsqrt(jnp.mean(xf * xf, axis=-1, keepdims=True) + EPS)
    return (y * g.astype(jnp.float32)).astype(x.dtype)


def swiglu(x, wg, wu, wd):
    return (jax.nn.silu(x @ wg) * (x @ wu)) @ wd


def t5_bucket(rel):
    half = N_BUCKETS // 2
    max_exact = half // 2
    ret = jnp.where(rel > 0, half, 0)
    n = jnp.abs(rel)
    nf = jnp.maximum(n, 1).astype(jnp.float32)
    large = max_exact + (jnp.log(nf / max_exact) / math.log(MAX_DISTANCE / max_exact)
                         * (half - max_exact)).astype(jnp.int32)
    large = jnp.minimum(large, half - 1)
    return ret + jnp.where(n < max_exact, n, large)


def dilated_group_attention(q, k, v, bias_table, dilation, side):
    B, S, H, hd = q.shape
    L = S // dilation
    nblk = -(-L // side)
    Lp = nblk * side

    def to_classes(t):
        t = t.reshape(B, L, dilation, H, hd).transpose(0, 2, 1, 3, 4)
        return jnp.pad(t, ((0, 0), (0, 0), (0, Lp - L), (0, 0), (0, 0)))

    qc, kc, vc = to_classes(q), to_classes(k), to_classes(v)
    qb = qc.reshape(B, dilation, nblk, side, H, hd)

    def windows(t):
        tp = jnp.pad(t, ((0, 0), (0, 0), (side, side), (0, 0), (0, 0)))
        tb = tp.reshape(B, dilation, nblk + 2, side, H, hd)
        return jnp.concatenate([tb[:, :, :-2], tb[:, :, 1:-1], tb[:, :, 2:]], axis=3)

    kw, vw = windows(kc), windows(vc)

    qi = jnp.arange(side)[:, None]
    kj = jnp.arange(3 * side)[None, :]
    delta = kj - side - qi
    kpos = jnp.arange(nblk)[:, None] * side + jnp.arange(3 * side)[None, :] - side
    valid = (kpos >= 0) & (kpos < L)
    mask = valid[:, None, :] & (jnp.abs(delta) <= side)[None]
    bias = bias_table.astype(jnp.float32)[t5_bucket(delta * dilation)].transpose(2, 0, 1)

    s = jnp.einsum('bcnqhd,bcnkhd->bcnhqk', qb, kw,
                   preferred_element_type=jnp.float32) * (HEAD_DIM ** -0.5)
    s = s + bias[None, None, None]
    s = jnp.where(mask[None, None, :, None], s, NEG_INF)
    m = jnp.max(s, axis=-1, keepdims=True)
    p = jnp.exp(s - m)
    l = jnp.sum(p, axis=-1, keepdims=True)
    o = jnp.einsum('bcnhqk,bcnkhd->bcnqhd', p / l, vw.astype(jnp.float32))
    lse = (m + jnp.log(l))[..., 0]

    o = o.reshape(B, dilation, Lp, H, hd)[:, :, :L].transpose(0, 2, 1, 3, 4).reshape(B, S, H, hd)
    lse = lse.transpose(0, 1, 2, 4, 3).reshape(B, dilation, Lp, H)[:, :, :L]
    lse = lse.transpose(0, 2, 1, 3).reshape(B, S, H)
    return o, lse


def multiscale_pool(u, w_pool, pool_scale):
    B, S, _ = u.shape
    uf = u.astype(jnp.float32)
    cs = jnp.pad(jnp.cumsum(uf, axis=1), ((0, 0), (1, 0), (0, 0)))
    pos = jnp.arange(S)
    diffs = []
    for g, w in enumerate(POOL_WINDOWS):
        h = w // 2
        lo = jnp.maximum(pos - h, 0)
        hi = jnp.minimum(pos + h + 1, S)
        seg = cs[:, :, g * POOL_GROUP:(g + 1) * POOL_GROUP]
        mean = (seg[:, hi] - seg[:, lo]) / (hi - lo).astype(jnp.float32)[None, :, None]
        diffs.append(mean - uf[..., g * POOL_GROUP:(g + 1) * POOL_GROUP])
    d = jnp.stack(diffs, axis=2).astype(u.dtype)
    y = jnp.einsum('bsgc,gce->bsge', d, w_pool).reshape(B, S, POOL_WIDTH)
    return y * pool_scale


def token_mixer(h, w_in, w_pool, pool_scale, w_out, rel_bias):
    B, S, _ = h.shape
    z = h @ w_in
    qkv = z[..., :QKV_WIDTH].reshape(B, S, N_GROUPS, 3, N_HEADS, HEAD_DIM)
    u = z[..., QKV_WIDTH:]
    outs, lses = [], []
    for g, (window, dil) in enumerate(DILATED_GROUPS):
        side = window // (2 * dil)
        o, lse = dilated_group_attention(qkv[:, :, g, 0], qkv[:, :, g, 1], qkv[:, :, g, 2],
                                         rel_bias[:, g * N_HEADS:(g + 1) * N_HEADS], dil, side)
        outs.append(o)
        lses.append(lse)
    wts = jax.nn.softmax(jnp.stack(lses), axis=0)
    att = jnp.einsum('gbsh,gbshd->bshd', wts, jnp.stack(outs)).reshape(B, S, ATT_WIDTH).astype(h.dtype)
    pool = multiscale_pool(u, w_pool, pool_scale).astype(h.dtype)
    return jnp.concatenate([att, pool], axis=-1) @ w_out


def trunk(x, norm_g, ffn_gate, ffn_up, ffn_down, w_in, w_pool, pool_scale, w_out, rel_bias, final_g):
    for l in range(DEPTH):
        x = x + 0.5 * swiglu(rmsnorm(x, norm_g[l, 0]), ffn_gate[l, 0], ffn_up[l, 0], ffn_down[l, 0])
        x = x + token_mixer(rmsnorm(x, norm_g[l, 1]), w_in[l], w_pool[l], pool_scale[l], w_out[l], rel_bias)
        x = x + 0.5 * swiglu(rmsnorm(x, norm_g[l, 2]), ffn_gate[l, 1], ffn_up[l, 1], ffn_down[l, 1])
    return rmsnorm(x, final_g)


def setup_inputs(seed: int = 0) -> dict:
    key = jax.random.key(seed)
    ks = jax.random.split(key, 12)
    f32 = jnp.float32
    x_prompt = jax.random.normal(ks[0], (BATCH, SEQ, D_MODEL), f32)
    x_sample = jax.random.normal(ks[1], (DEC_BATCH, DEC_SEQ, D_MODEL), f32)
    norm_g = 1.0 + 0.02 * jax.random.normal(ks[2], (DEPTH, 3, D_MODEL), f32)
    ffn_gate = jax.random.normal(ks[3], (DEPTH, 2, D_MODEL, D_FF), f32) * D_MODEL ** -0.5
    ffn_up = jax.random.normal(ks[4], (DEPTH, 2, D_MODEL, D_FF), f32) * D_MODEL ** -0.5
    ffn_down = jax.random.normal(ks[5], (DEPTH, 2, D_FF, D_MODEL), f32) * D_FF ** -0.5
    w_in = jax.random.normal(ks[6], (DEPTH, D_MODEL, IN_WIDTH), f32) * D_MODEL ** -0.5
    w_pool = jax.random.normal(ks[7], (DEPTH, len(POOL_WINDOWS), POOL_GROUP, POOL_GROUP), f32) * POOL_GROUP ** -0.5
    pool_scale = 1.0 + 0.02 * jax.random.normal(ks[8], (DEPTH, POOL_WIDTH), f32)
    w_out = jax.random.normal(ks[9], (DEPTH, D_MODEL, D_MODEL), f32) * D_MODEL ** -0.5
    rel_bias = 0.3 * jax.random.normal(ks[10], (N_BUCKETS, N_GROUPS * N_HEADS), f32)
    final_g = 1.0 + 0.02 * jax.random.normal(ks[11], (D_MODEL,), f32)
    return {"x_prompt": x_prompt, "x_sample": x_sample, "norm_g": norm_g, "ffn_gate": ffn_gate,
            "ffn_up": ffn_up, "ffn_down": ffn_down, "w_in": w_in, "w_pool": w_pool,
            "pool_scale": pool_scale, "w_out": w_out, "rel_bias": rel_bias, "final_g": final_g}


def reference(x_prompt, x_sample, norm_g, ffn_gate, ffn_up, ffn_down, w_in, w_pool, pool_scale,
              w_out, rel_bias, final_g):
    y_prompt = trunk(x_prompt, norm_g, ffn_gate, ffn_up, ffn_down, w_in, w_pool, pool_scale,
                     w_out, rel_bias, final_g)
    y_sample = trunk(x_sample, norm_g, ffn_gate, ffn_up, ffn_down, w_in, w_pool, pool_scale,
                     w_out, rel_bias, final_g)
    return (y_prompt, y_sample)
```

```python
import math
from contextlib import ExitStack

import numpy as np
import concourse.bass as bass
import concourse.mybir as mybir
from concourse.bass_utils import run_bass_kernel_spmd

F32 = mybir.dt.float32
BF16 = mybir.dt.bfloat16
ALU = mybir.AluOpType
AF = mybir.ActivationFunctionType

EPS = 1e-6
N_BUCKETS = 32
MAX_DISTANCE = 1024
DILS = (1, 4, 16)
POOL_H = (1, 2, 4, 8)
SEM_LIMIT = 24000
NSLOT = 3 * 128
SELR = 48


class Cfg:
    def __init__(self, NH=12, F=5632, L=4, NSEQ=3, T=2048, TT=1024):
        self.NH = NH
        self.DC = NH + 4
        self.D = 128 * self.DC
        self.F = F
        self.FC = F // 128
        self.L = L
        self.NSEQ = NSEQ
        self.T = T
        self.TT = TT
        self.NT = T // TT
        self.NU = NH * 3


class Buf:
    __slots__ = ("name", "writers", "readers")

    def __init__(self, name):
        self.name = name
        self.writers = []
        self.readers = []


class DmaSem:
    __slots__ = ("h", "count", "last")

    def __init__(self, h):
        self.h = h
        self.count = 0
        self.last = None


class Op:
    __slots__ = ("eng", "fn", "deps", "dma", "waited", "tok")

    def __init__(self, eng, fn, dma):
        self.eng = eng
        self.fn = fn
        self.dma = dma
        self.deps = ()
        self.waited = False
        self.tok = None


ENGS = ("pe", "act", "dve", "pool", "sp")
COMPUTE = ("pe", "act", "dve")


class Prog:
    def __init__(self):
        self.ops = {e: [] for e in ENGS}
        self.dma_sems = []

    def op(self, eng, fn, reads=(), writes=(), dma=None, extra=()):
        o = Op(eng, fn, dma)
        deps = set(extra)
        for b in reads:
            deps.update(b.writers)
        for b in writes:
            deps.update(b.writers)
            deps.update(b.readers)
        if eng == "pe":
            deps = {d for d in deps if d.eng != "pe"}
        o.deps = deps
        for d in deps:
            d.waited = True
        for b in reads:
            b.readers.append(o)
        for b in writes:
            b.writers = [o]
            b.readers = []
        if dma is not None:
            dma.last = o
            o.waited = True
            dma.count += 16
            o.tok = (dma.h, dma.count, 16)
        self.ops[eng].append(o)
        return o

    def barrier(self):
        lasts = []
        for e in COMPUTE:
            if self.ops[e]:
                for o in reversed(self.ops[e]):
                    if o.fn is not None:
                        lasts.append(o)
                        break
        for s in self.dma_sems:
            if s.last is not None:
                lasts.append(s.last)
        for e in ENGS:
            o = Op(e, None, None)
            o.deps = set(lasts)
            for d in lasts:
                d.waited = True
            self.ops[e].append(o)

    def finalize(self, sem_pool):
        for e in COMPUTE:
            cur = sem_pool.pop()
            cnt = 0
            for o in self.ops[e]:
                if o.fn is None or not o.waited:
                    continue
                if cnt >= SEM_LIMIT:
                    cur = sem_pool.pop()
                    cnt = 0
                cnt += 1
                o.tok = (cur, cnt, 1)
        for e in ("pool", "sp"):
            for o in self.ops[e]:
                assert o.fn is None or o.tok is not None, "DMA op without semaphore"

    def emit(self, eng, e):
        known = {}
        for o in self.ops[eng]:
            need = {}
            for d in o.deps:
                h, v, _ = d.tok
                k = h.num
                if need.get(k, (None, 0))[1] < v:
                    need[k] = (h, v)
            for k, (h, v) in need.items():
                if known.get(k, 0) >= v:
                    continue
                known[k] = v
                e.wait_ge(h, v)
            if o.fn is None:
                continue
            r = o.fn(e)
            last = r[-1] if isinstance(r, (list, tuple)) else r
            if o.tok is not None:
                last.then_inc(o.tok[0], o.tok[2])


class Arena:
    def __init__(self, tensor, nbytes):
        self.t = tensor
        self.n = nbytes
        self.off = 0
        self.mark = 0

    def alloc(self, cols, dtype):
        sz = 4 if dtype == F32 else 2
        nb = (cols * sz + 31) // 32 * 32
        assert self.off + nb <= self.n, f"arena overflow {self.off}+{nb}>{self.n}"
        a = self.t[:, self.off // 4:(self.off + nb) // 4]
        self.off += nb
        if dtype != F32:
            a = a.bitcast(dtype)
        return a[:, 0:cols]

    def set_mark(self):
        self.mark = self.off

    def reset(self):
        self.off = self.mark


def _t5_bucket_np(rel):
    half = N_BUCKETS // 2
    max_exact = half // 2
    ret = np.where(rel > 0, half, 0)
    n = np.abs(rel)
    nf = np.maximum(n, 1).astype(np.float32)
    large = max_exact + (np.log(nf / np.float32(max_exact)) / np.float32(math.log(MAX_DISTANCE / max_exact))
                         * np.float32(half - max_exact)).astype(np.int32)
    large = np.minimum(large, half - 1)
    return ret + np.where(n < max_exact, n, large)


def _sel_const():
    k = np.arange(128)[:, None, None]
    j = np.arange(3)[None, :, None]
    q = np.arange(128)[None, None, :]
    delta = k + 128 * (j - 1) - q
    out = np.zeros((3, SELR, 128 * NSLOT), np.float32)
    for g, dil in enumerate(DILS):
        b = _t5_bucket_np((delta * dil).astype(np.int32))
        b = np.where(np.abs(delta) <= 64, b, 32).reshape(-1)
        out[g, b, np.arange(b.size)] = 1.0
    return out


def _invc_const(T):
    pos = np.arange(T)
    out = np.zeros((4, 128, T), np.float32)
    for g, h in enumerate(POOL_H):
        lo = np.maximum(pos - h, 0)
        hi = np.minimum(pos + h + 1, T)
        out[g] = (np.float32(1.0) / (hi - lo).astype(np.float32))[None, :]
    return out


def build_program(cfg):
    C = cfg
    D, DC, F, FC, L, T, TT, NT, NH, NU, NSEQ = C.D, C.DC, C.F, C.FC, C.L, C.T, C.TT, C.NT, C.NH, C.NU, C.NSEQ
    NQ = T // 512
    NB = T // 128

    nc = bass.Bass("TRN2", target_bir_lowering=False)

    def din(name, shape, dt=F32):
        return nc.dram_tensor(name, list(shape), dt, kind="ExternalInput").ap()

    xin_d = din("xin", [NSEQ * D, T])
    wgu_d = din("wgu", [L * 2 * FC * 128, 2 * DC * 128])
    wd_d = din("wd", [L * 2 * DC * 128, FC * 128])
    wqkv_d = din("wqkv", [L * NU * 128, DC * 3 * 128])
    wpu_d = din("wpu", [L * 4 * 128, DC * 128])
    wo_d = din("wo", [L * DC * 128, DC * 128])
    wpool_d = din("wpool", [L * 128, 4 * 128])
    gvec_d = din("gvec", [128, L * 3 * DC])
    fg_d = din("fg", [128, DC])
    pscale_d = din("pscale", [128, L * 4])
    rb_d = din("rb", [32, 3 * NH])
    sel_d = din("sel", [3 * SELR, 128 * NSLOT])
    invc_d = din("invc", [4 * 128, T])
    ident_d = din("ident", [128, 128])
    yT_d = nc.dram_tensor("yT", [NSEQ * D, T], F32, kind="ExternalOutput").ap()
    xs_d = nc.dram_tensor("xs", [NSEQ * D, T], F32, kind="Internal").ap()
    att_d = nc.dram_tensor("attT", [NSEQ * D, T], BF16, kind="Internal").ap()
    expb_d = nc.dram_tensor("expb", [NU, 128 * NSLOT], F32, kind="Internal").ap()

    P = Prog()
    es = ExitStack()
    with es:
        arena_t = es.enter_context(nc.sbuf_tensor("arena", [128, 212000 // 4], F32))
        AR = Arena(arena_t, 212000)
        banks = [es.enter_context(nc.psum_tensor(f"bank{i}", [128, 512], F32)) for i in range(8)]
        bankB = [Buf(f"bank{i}") for i in range(8)]
        sem_pool = [es.enter_context(nc.semaphore(f"s{i}")) for i in range(100)]

        free_ds = []
        inuse_ds = []

        def dsem():
            if free_ds:
                s = free_ds.pop(0)
            else:
                s = DmaSem(sem_pool.pop())
                P.dma_sems.append(s)
            inuse_ds.append(s)
            return s

        def phase_begin():
            P.barrier()
            while inuse_ds:
                s = inuse_ds.pop(0)
                if s.count > 12000:
                    s.h = sem_pool.pop()
                    s.count = 0
                    s.last = None
                free_ds.append(s)
            AR.reset()

        ones_bf = AR.alloc(128, BF16)
        ident_bf = AR.alloc(128, BF16)
        gvec = AR.alloc(L * 3 * DC, F32)
        fg = AR.alloc(DC, F32)
        pscale = AR.alloc(L * 4, F32)
        rstd = AR.alloc(T, F32)
        epsc = AR.alloc(8, F32)
        constB = Buf("const")
        rstdB = [Buf(f"rstd{t}") for t in range(NT)]
        AR.set_mark()

        xB = [[[Buf(f"x{s}_{t}_{c}") for c in range(DC)] for t in range(NT)] for s in range(NSEQ)]
        attB = [[Buf(f"att{s}_{c}") for c in range(DC)] for s in range(NSEQ)]
        expbB = [Buf(f"expb{g}") for g in range(3)]
        outB = Buf("out")

        def setup():
            cs = dsem()
            P.op("dve", lambda e: e.memset(ones_bf, 1.0), writes=[constB])
            P.op("dve", lambda e: e.memset(epsc, EPS), writes=[constB])
            P.op("pool", lambda e: e.dma_start(out=ident_bf, in_=ident_d[:, :]), writes=[constB], dma=cs)
            cs2 = dsem()
            P.op("sp", lambda e: e.dma_start(out=gvec, in_=gvec_d[:, :]), writes=[constB], dma=cs2)
            cs3 = dsem()
            P.op("sp", lambda e: e.dma_start(out=fg, in_=fg_d[:, :]), writes=[constB], dma=cs3)
            cs4 = dsem()
            P.op("sp", lambda e: e.dma_start(out=pscale, in_=pscale_d[:, :]), writes=[constB], dma=cs4)
            rb = AR.alloc(3 * NH, F32)
            rbB = Buf("rb")
            P.op("dve", lambda e: e.memset(rb[0:64, :], -30000.0), writes=[rbB])
            cs5 = dsem()
            P.op("sp", lambda e: e.dma_start(out=rb[0:32, :], in_=rb_d[:, :]), writes=[rbB], dma=cs5)
            PC = 1024
            NSL = 4
            selS = [AR.alloc(PC, F32) for _ in range(NSL)]
            selB = [Buf(f"sel{i}") for i in range(NSL)]
            selD = [dsem() for _ in range(NSL)]
            stg = [AR.alloc(PC, F32) for _ in range(NSL)]
            stgB = [Buf(f"stg{i}") for i in range(NSL)]
            stgD = [dsem() for _ in range(NSL)]
            it = 0
            for g in range(3):
                for pc in range(128 * NSLOT // PC):
                    sl = it % NSL
                    it += 1
                    P.op("sp", lambda e, g=g, pc=pc, sl=sl: e.dma_start(
                        out=selS[sl][0:SELR, :], in_=sel_d[g * SELR:(g + 1) * SELR, pc * PC:(pc + 1) * PC]),
                        writes=[selB[sl]], dma=selD[sl])
                    bk = [2 * sl + i for i in range(2)]

                    def mm(e, g=g, sl=sl, bk=bk):
                        r = []
                        for i in range(2):
                            r.append(e.matmul(banks[bk[i]][0:NH, :], rb[0:SELR, g * NH:(g + 1) * NH],
                                              selS[sl][0:SELR, i * 512:(i + 1) * 512], start=True, stop=True))
                        return r
                    P.op("pe", mm, reads=[rbB, selB[sl]], writes=[bankB[b] for b in bk])

                    def ex(e, sl=sl, bk=bk):
                        r = []
                        for i in range(2):
                            r.append(e.activation(stg[sl][0:NH, i * 512:(i + 1) * 512], banks[bk[i]][0:NH, :], AF.Exp))
                        return r
                    P.op("act", ex, reads=[bankB[b] for b in bk], writes=[stgB[sl]])
                    dst = expb_d.rearrange("(h g) n -> h g n", g=3)[:, g, pc * PC:(pc + 1) * PC]
                    P.op("pool", lambda e, sl=sl, dst=dst: e.dma_start(out=dst, in_=stg[sl][0:NH, :]),
                         reads=[stgB[sl]], writes=[expbB[g]], dma=stgD[sl])

        class XIO:
            def __init__(self, nin=3, nst=2, nsq=2):
                self.xin = [AR.alloc(TT, F32) for _ in range(nin)]
                self.xinB = [Buf(f"xin{i}") for i in range(nin)]
                self.xinD = [dsem() for _ in range(nin)]
                self.st = [AR.alloc(TT, F32) for _ in range(nst)]
                self.stB = [Buf(f"st{i}") for i in range(nst)]
                self.stD = [dsem() for _ in range(nst)]
                self.sq = [AR.alloc(TT, BF16) for _ in range(nsq)]
                self.sqB = [Buf(f"sq{i}") for i in range(nsq)]
                self.i_in = 0
                self.i_st = 0
                self.i_sq = 0

            def load(self, src_d, s, t, c):
                sl = self.i_in % len(self.xin)
                self.i_in += 1
                src = src_d[s * D + c * 128:s * D + (c + 1) * 128, t * TT:(t + 1) * TT]
                self.last_load = P.op("sp", lambda e: e.dma_start(out=self.xin[sl], in_=src),
                                      reads=[xB[s][t][c]], writes=[self.xinB[sl]], dma=self.xinD[sl])
                return sl

            def stage(self):
                sl = self.i_st % len(self.st)
                self.i_st += 1
                return sl

            def sqslot(self):
                sl = self.i_sq % len(self.sq)
                self.i_sq += 1
                return sl

        def stats_mm(io, sq, c, ST, STB):
            def f(e):
                r = []
                for hf in range(TT // 512):
                    r.append(e.matmul(banks[ST[hf]][:, :], ones_bf, io.sq[sq][:, hf * 512:(hf + 1) * 512],
                                      start=(c == 0), stop=(c == DC - 1)))
                return r
            P.op("pe", f, reads=[io.sqB[sq], constB], writes=STB)

        def stats_finish(io, t, ST, STB, tmpf, tmpfB):
            def f(e):
                r = []
                for hf in range(TT // 512):
                    r.append(e.activation(tmpf[:, hf * 512:(hf + 1) * 512], banks[ST[hf]][:, :], AF.Ln,
                                          bias=epsc[:, 0:1], scale=1.0 / D))
                return r
            P.op("act", f, reads=STB + [constB], writes=[tmpfB])
            P.op("act", lambda e: e.activation(rstd[:, t * TT:(t + 1) * TT], tmpf, AF.Exp, scale=-0.5),
                 reads=[tmpfB], writes=[rstdB[t]])

        def stats_prologue(s):
            phase_begin()
            io = XIO()
            tmpf = AR.alloc(TT, F32)
            tmpfB = Buf("tmpf")
            ST = [4, 5]
            STB = [bankB[4], bankB[5]]
            for t in range(NT):
                for c in range(DC):
                    sl = io.load(xin_d, s, t, c)
                    q = io.sqslot()
                    P.op("act", lambda e, sl=sl, q=q: e.activation(io.sq[q], io.xin[sl], AF.Square),
                         reads=[io.xinB[sl]], writes=[io.sqB[q]])
                    stats_mm(io, q, c, ST, STB)
                stats_finish(io, t, ST, STB, tmpf, tmpfB)

        def final_norm(s):
            phase_begin()
            io = XIO()
            for t in range(NT):
                for c in range(DC):
                    sl = io.load(xs_d, s, t, c)
                    st = io.stage()
                    P.op("dve", lambda e, sl=sl, st=st, c=c, t=t: e.scalar_tensor_tensor(
                        io.st[st], io.xin[sl], fg[:, c:c + 1], rstd[:, t * TT:(t + 1) * TT], ALU.mult, ALU.mult),
                        reads=[io.xinB[sl], rstdB[t], constB], writes=[io.stB[st]])
                    dst = yT_d[s * D + c * 128:s * D + (c + 1) * 128, t * TT:(t + 1) * TT]
                    P.op("sp", lambda e, st=st, dst=dst: e.dma_start(out=dst, in_=io.st[st]),
                         reads=[io.stB[st]], writes=[outB], dma=io.stD[st])

        def make_h(io, src_d, s, t, gcol, hT, hTB, toff):
            for c in range(DC):
                sl = io.load(src_d, s, t, c)
                P.op("dve", lambda e, sl=sl, c=c: e.scalar_tensor_tensor(
                    hT[:, c, toff:toff + TT], io.xin[sl], gvec[:, gcol + c:gcol + c + 1],
                    rstd[:, t * TT:(t + 1) * TT], ALU.mult, ALU.mult),
                    reads=[io.xinB[sl], rstdB[t], constB], writes=[hTB[c]])

        def proj(io, s, t, src_d, actT, actB, KC, w_d, wrow0, wS, wSB, wSD, scale, tmpf, tmpfB):
            Y = [[0, 1], [2, 3]]
            ST = [4, 5]
            STB = [bankB[4], bankB[5]]
            pend = None
            for dc in range(DC):
                ws = dc % 2
                par = dc % 2
                wsrc = w_d[wrow0 + dc * 128:wrow0 + (dc + 1) * 128, :]
                P.op("pool", lambda e, ws=ws, wsrc=wsrc: e.dma_start(out=wS[ws][:, 0:KC * 128], in_=wsrc),
                     writes=[wSB[ws]], dma=wSD[ws])
                sl = io.load(src_d, s, t, dc)
                yb = Y[par]

                def mm(e, ws=ws, yb=yb):
                    r = []
                    wv = wS[ws][:, 0:KC * 128].rearrange("p (k d) -> p k d", d=128)
                    for hf in range(TT // 512):
                        for k in range(KC):
                            r.append(e.matmul(banks[yb[hf]][:, :], wv[:, k, :], actT[:, k, hf * 512:(hf + 1) * 512],
                                              start=(k == 0), stop=(k == KC - 1)))
                    return r
                P.op("pe", mm, reads=[wSB[ws]] + list(actB), writes=[bankB[b] for b in yb])
                if pend is not None:
                    stats_mm(io, pend[0], pend[1], ST, STB)
                st = io.stage()

                def ep(e, sl=sl, st=st, yb=yb):
                    r = []
                    for hf in range(TT // 512):
                        r.append(e.scalar_tensor_tensor(io.st[st][:, hf * 512:(hf + 1) * 512], banks[yb[hf]][:, :],
                                                        float(scale), io.xin[sl][:, hf * 512:(hf + 1) * 512],
                                                        ALU.mult, ALU.add))
                    return r
                P.op("dve", ep, reads=[bankB[b] for b in yb] + [io.xinB[sl]], writes=[io.stB[st]])
                dst = xs_d[s * D + dc * 128:s * D + (dc + 1) * 128, t * TT:(t + 1) * TT]
                P.op("sp", lambda e, st=st, dst=dst: e.dma_start(out=dst, in_=io.st[st]),
                     reads=[io.stB[st]], writes=[xB[s][t][dc]], dma=io.stD[st])
                q = io.sqslot()
                P.op("act", lambda e, st=st, q=q: e.activation(io.sq[q], io.st[st], AF.Square),
                     reads=[io.stB[st]], writes=[io.sqB[q]])
                pend = (q, dc)
            stats_mm(io, pend[0], pend[1], ST, STB)
            stats_finish(io, t, ST, STB, tmpf, tmpfB)

        def ffn_chain(s, items, wout_l=None):
            phase_begin()
            io = XIO()
            hT = AR.alloc(DC * TT, BF16).rearrange("p (c t) -> p c t", t=TT)
            hTB = [Buf(f"hT{c}") for c in range(DC)]
            aT = AR.alloc(FC * TT, BF16).rearrange("p (c t) -> p c t", t=TT)
            aTB = [Buf(f"aT{c}") for c in range(FC)]
            wgu = [AR.alloc(2 * DC * 128, BF16) for _ in range(2)]
            wguB = [Buf("wgu0"), Buf("wgu1")]
            wguD = [dsem(), dsem()]
            wd = [AR.alloc(FC * 128, BF16) for _ in range(2)]
            wdB = [Buf("wd0"), Buf("wd1")]
            wdD = [dsem(), dsem()]
            tmp = [AR.alloc(TT, F32) for _ in range(2)]
            tmpB = [Buf("tmp0"), Buf("tmp1")]
            tmpf = AR.alloc(TT, F32)
            tmpfB = Buf("tmpf")
            GB = [[0, 1], [4, 5]]
            UB = [[2, 3], [6, 7]]
            jobs = [(l, i, src_d, t) for (l, i, src_d) in items for t in range(NT)]

            def do_h(job):
                l, i, src_d, t = job
                gcol = (l * 3 + (0 if i == 0 else 2)) * DC
                make_h(io, src_d, s, t, gcol, hT, hTB, 0)

            def do_gu(job, first_extra):
                l, i, src_d, t = job
                for fc in range(FC):
                    ws = fc % 2
                    par = fc % 2
                    row = ((l * 2 + i) * FC + fc) * 128
                    wsrc = wgu_d[row:row + 128, :]
                    P.op("pool", lambda e, ws=ws, wsrc=wsrc: e.dma_start(out=wgu[ws], in_=wsrc),
                         writes=[wguB[ws]], dma=wguD[ws], extra=(first_extra if fc == 0 else ()))
                    gb, ub = GB[par], UB[par]

                    def mm(e, ws=ws, gb=gb, ub=ub):
                        r = []
                        wv = wgu[ws].rearrange("p (j c f) -> p j c f", j=2, f=128)
                        for j, bb in ((0, gb), (1, ub)):
                            for hf in range(TT // 512):
                                for c in range(DC):
                                    r.append(e.matmul(banks[bb[hf]][:, :], wv[:, j, c, :], hT[:, c, hf * 512:(hf + 1) * 512],
                                                      start=(c == 0), stop=(c == DC - 1)))
                        return r
                    P.op("pe", mm, reads=[wguB[ws]] + hTB, writes=[bankB[b] for b in gb + ub])

                    def sil(e, par=par, gb=gb):
                        return [e.activation(tmp[par][:, hf * 512:(hf + 1) * 512], banks[gb[hf]][:, :], AF.Silu)
                                for hf in range(TT // 512)]
                    P.op("act", sil, reads=[bankB[b] for b in gb], writes=[tmpB[par]])

                    def mul(e, par=par, ub=ub, fc=fc):
                        return [e.tensor_tensor(aT[:, fc, hf * 512:(hf + 1) * 512], banks[ub[hf]][:, :],
                                                tmp[par][:, hf * 512:(hf + 1) * 512], ALU.mult)
                                for hf in range(TT // 512)]
                    P.op("dve", mul, reads=[bankB[b] for b in ub] + [tmpB[par]], writes=[aTB[fc]])

            def do_down(job):
                l, i, src_d, t = job
                proj(io, s, t, src_d, aT, aTB, FC, wd_d, (l * 2 + i) * DC * 128, wd, wdB, wdD, 0.5, tmpf, tmpfB)

            first_extra = ()
            if wout_l is not None:
                assert FC >= NT * DC
                atD = [dsem() for _ in range(NT)]
                for t in range(NT):
                    src = att_d[s * D:(s + 1) * D, t * TT:(t + 1) * TT].rearrange("(c p) t -> p c t", p=128)
                    P.op("sp", lambda e, src=src, t=t: e.dma_start(out=aT[:, t * DC:(t + 1) * DC, :], in_=src),
                         reads=attB[s], writes=aTB[t * DC:(t + 1) * DC], dma=atD[t])
                for t in range(NT):
                    proj(io, s, t, xs_d, aT[:, t * DC:(t + 1) * DC, :], aTB[t * DC:(t + 1) * DC], DC, wo_d,
                         wout_l * DC * 128, wd, wdB, wdD, 1.0, tmpf, tmpfB)
                    if t == 0:
                        do_h(jobs[0])
            else:
                do_h(jobs[0])
                first_extra = (io.last_load,)
            for n, job in enumerate(jobs):
                do_gu(job, first_extra if n == 0 else ())
                if n + 1 < len(jobs):
                    do_h(jobs[n + 1])
                do_down(job)

        def mixer(s, l, src_d):
            phase_begin()
            io = XIO(nin=2, nst=1, nsq=1)
            hT = AR.alloc(DC * T, BF16).rearrange("p (c t) -> p c t", t=T)
            hTBt = [[Buf(f"mhT{t}_{c}") for c in range(DC)] for t in range(NT)]
            hTB = [b for row in hTBt for b in row]
            ao = [AR.alloc(T, BF16) for _ in range(2)]
            aoB = [Buf("ao0"), Buf("ao1")]
            aoD = [dsem(), dsem()]
            gcol = (l * 3 + 1) * DC
            for t in range(NT):
                make_h(io, src_d, s, t, gcol, hT, hTBt[t], t * TT)
            mix_first = [io.last_load]
            PJ = [0, 1]
            cnt = {"pj": 0, "tr": 0, "blk": 0, "ev": 0, "ao": 0}
            sub_mark = AR.off

            TP = T + 16
            Up = AR.alloc(TP, F32)
            sA = AR.alloc(TP, F32)
            sBf = AR.alloc(TP, F32)
            dT = AR.alloc(T, BF16)
            UpB, sAB, sBB, dTB = Buf("Up"), Buf("sA"), Buf("sB"), Buf("dT")
            wpu = [AR.alloc(DC * 128, BF16) for _ in range(2)]
            wpuB = [Buf("wpu0"), Buf("wpu1")]
            wpuD = [dsem(), dsem()]
            wpl = AR.alloc(4 * 128, BF16)
            wplB = Buf("wpl")
            wplD = dsem()
            ivc = AR.alloc(T, F32)
            ivcB = Buf("ivc")
            ivcD = dsem()
            P.op("dve", lambda e: e.memset(Up, 0.0), writes=[UpB])
            P.op("pool", lambda e: e.dma_start(out=wpl, in_=wpool_d[l * 128:(l + 1) * 128, :]), writes=[wplB], dma=wplD,
                 extra=mix_first)
            for pg in range(4):
                ws = pg % 2
                row = (l * 4 + pg) * 128
                P.op("pool", lambda e, ws=ws, row=row: e.dma_start(out=wpu[ws], in_=wpu_d[row:row + 128, :]),
                     writes=[wpuB[ws]], dma=wpuD[ws])
                P.op("sp", lambda e, pg=pg: e.dma_start(out=ivc, in_=invc_d[pg * 128:(pg + 1) * 128, :]),
                     writes=[ivcB], dma=ivcD)
                for tq in range(NQ):
                    bk = PJ[cnt["pj"] % 2]
                    cnt["pj"] += 1

                    def mm(e, ws=ws, tq=tq, bk=bk):
                        wv = wpu[ws].rearrange("p (c f) -> p c f", f=128)
                        return [e.matmul(banks[bk][:, :], wv[:, c, :], hT[:, c, tq * 512:(tq + 1) * 512],
                                         start=(c == 0), stop=(c == DC - 1)) for c in range(DC)]
                    P.op("pe", mm, reads=[wpuB[ws]] + hTBt[(tq * 512) // TT], writes=[bankB[bk]])
                    P.op("act", lambda e, tq=tq, bk=bk: e.activation(Up[:, 8 + tq * 512:8 + (tq + 1) * 512],
                                                                    banks[bk][:, :], AF.Copy),
                         reads=[bankB[bk]], writes=[UpB])
                hh = POOL_H[pg]
                src, srcB = Up, UpB
                pp = [(sA, sAB), (sBf, sBB)]
                pi = 0
                kk = 1
                while kk <= hh:
                    n = TP - (2 * kk - 1)
                    dst, dstB = pp[pi]
                    pi ^= 1
                    P.op("dve", lambda e, dst=dst, src=src, n=n, kk=kk: e.tensor_tensor(
                        dst[:, 0:n], src[:, 0:n], src[:, kk:kk + n], ALU.add), reads=[srcB], writes=[dstB])
                    src, srcB = dst, dstB
                    kk *= 2
                dst, dstB = pp[pi]
                pi ^= 1
                P.op("dve", lambda e, dst=dst, src=src, hh=hh: e.tensor_tensor(
                    dst[:, 0:T], src[:, 8 - hh:8 - hh + T], Up[:, 8 + hh:8 + hh + T], ALU.add),
                    reads=[srcB, UpB], writes=[dstB])
                w_, wB_ = dst, dstB
                dst, dstB = pp[pi]
                P.op("dve", lambda e, dst=dst, w_=w_: e.tensor_tensor(dst[:, 0:T], w_[:, 0:T], ivc, ALU.mult),
                     reads=[wB_, ivcB], writes=[dstB])
                P.op("dve", lambda e, dst=dst: e.tensor_tensor(dT, dst[:, 0:T], Up[:, 8:8 + T], ALU.subtract),
                     reads=[dstB, UpB], writes=[dTB])
                k = cnt["ao"] % 2
                cnt["ao"] += 1
                for tq in range(NQ):
                    bk = PJ[cnt["pj"] % 2]
                    cnt["pj"] += 1
                    P.op("pe", lambda e, pg=pg, tq=tq, bk=bk: e.matmul(
                        banks[bk][:, :], wpl[:, pg * 128:(pg + 1) * 128], dT[:, tq * 512:(tq + 1) * 512],
                        start=True, stop=True), reads=[wplB, dTB], writes=[bankB[bk]])
                    P.op("act", lambda e, pg=pg, tq=tq, bk=bk, k=k: e.activation(
                        ao[k][:, tq * 512:(tq + 1) * 512], banks[bk][:, :], AF.Copy,
                        scale=pscale[:, l * 4 + pg:l * 4 + pg + 1]),
                        reads=[bankB[bk], constB], writes=[aoB[k]])
                dst = att_d[s * D + (NH + pg) * 128:s * D + (NH + pg + 1) * 128, :]
                P.op("sp", lambda e, k=k, dst=dst: e.dma_start(out=dst, in_=ao[k]),
                     reads=[aoB[k]], writes=[attB[s][NH + pg]], dma=aoD[k])


            P.barrier()
            AR.off = sub_mark
            wq = [AR.alloc(DC * 3 * 128, BF16) for _ in range(2)]
            wqB = [Buf("wq0"), Buf("wq1")]
            wqD = [dsem(), dsem()]
            qkv = [AR.alloc(3 * T, BF16).rearrange("p (j t) -> p j t", t=T) for _ in range(2)]
            qkvB = [[Buf(f"qkv{a}_{j}") for j in range(3)] for a in range(2)]
            Vt = [AR.alloc(NB * 128, BF16).rearrange("p (k d) -> p k d", d=128) for _ in range(2)]
            VtB = [Buf("V0"), Buf("V1")]
            eb = [AR.alloc(NSLOT, F32) for _ in range(2)]
            ebB = [Buf("eb0"), Buf("eb1")]
            ebD = [dsem(), dsem()]
            et = [AR.alloc(NSLOT, F32) for _ in range(2)]
            etB = [Buf("et0"), Buf("et1")]
            pt = [AR.alloc(NSLOT, BF16) for _ in range(2)]
            ptB = [Buf("pt0"), Buf("pt1")]
            accO2 = [AR.alloc(T, F32) for _ in range(2)]
            accL2 = [AR.alloc(T, F32) for _ in range(2)]
            accOB = [Buf(f"accO{b}") for b in range(2)]
            accLB = [Buf(f"accL{b}") for b in range(2)]

            TR = [2, 7]
            SB = [3, 4]
            OL = [5, 6]

            def load_unit_w(u):
                ws = u % 2
                row = (l * NU + u) * 128
                wsrc = wqkv_d[row:row + 128, :]
                P.op("pool", lambda e: e.dma_start(out=wq[ws], in_=wsrc), writes=[wqB[ws]], dma=wqD[ws])
                src = expb_d[u].rearrange("(k n) -> k n", n=NSLOT)
                P.op("sp", lambda e: e.dma_start(out=eb[ws], in_=src), reads=[expbB[u % 3]],
                     writes=[ebB[ws]], dma=ebD[ws])

            def pj_groups(u):
                a = u % 2
                g = u % 3
                dil = DILS[g]
                Lc = T // dil
                out = []
                for tq in range(NQ):
                    for j in range(3):
                        def rec(tq=tq, j=j):
                            bk = PJ[cnt["pj"] % 2]
                            cnt["pj"] += 1
                            wv = wq[a].rearrange("p (c j f) -> p c j f", j=3, f=128)

                            def mm(e):
                                return [e.matmul(banks[bk][:, :], wv[:, c, j, :], hT[:, c, tq * 512:(tq + 1) * 512],
                                                 start=(c == 0), stop=(c == DC - 1)) for c in range(DC)]
                            P.op("pe", mm, reads=[wqB[a]] + hTBt[(tq * 512) // TT], writes=[bankB[bk]])
                            ni = 512 // dil
                            i0 = tq * ni
                            if dil == 1:
                                dst = qkv[a][:, j, tq * 512:(tq + 1) * 512]
                                srcv = banks[bk][:, :]
                            else:
                                dst = qkv[a][:, j, :].rearrange("p (c i) -> p c i", c=dil)[:, :, i0:i0 + ni]
                                dst = dst.rearrange("p c i -> p i c")
                                srcv = banks[bk][:, :].rearrange("p (i c) -> p i c", c=dil)
                            if (cnt["ev"] % 2) == 0:
                                P.op("act", lambda e: e.activation(dst, srcv, AF.Copy),
                                     reads=[bankB[bk]], writes=[qkvB[a][j]])
                            else:
                                P.op("dve", lambda e: e.tensor_copy(dst, srcv),
                                     reads=[bankB[bk]], writes=[qkvB[a][j]])
                            cnt["ev"] += 1
                        out.append(rec)
                return out

            def tr_groups(u):
                a = u % 2
                return [lambda grp=grp: tr_group(a, grp) for grp in range(NB // 4)]

            def tr_group(a, grp):
                if True:
                    bk = TR[cnt["tr"] % 2]
                    cnt["tr"] += 1
                    pb = banks[bk][:, :].bitcast(BF16)

                    def tr(e, grp=grp, pb=pb):
                        return [e.transpose(pb[:, i * 128:(i + 1) * 128],
                                            qkv[a][:, 2, (grp * 4 + i) * 128:(grp * 4 + i + 1) * 128], ident_bf)
                                for i in range(4)]
                    P.op("pe", tr, reads=[qkvB[a][2], constB], writes=[bankB[bk]])
                    dst = Vt[a][:, grp * 4:(grp + 1) * 4, :]
                    srcv = pb[:, 0:512].rearrange("p (k d) -> p k d", d=128)
                    P.op("dve", lambda e, dst=dst, srcv=srcv: e.tensor_copy(dst, srcv),
                         reads=[bankB[bk]], writes=[VtB[a]])

            def valid_slots(g, b):
                bpc = (T // DILS[g]) // 128
                return [j for j in range(3) if 0 <= b + j - 1 < NB and (b + j - 1) // bpc == b // bpc]

            def s_block(u, b):
                a = u % 2
                g = u % 3
                k = cnt["blk"] % 2
                js = valid_slots(g, b)
                sbk = SB[k]

                def mm(e):
                    return [e.matmul(banks[sbk][:, j * 128:(j + 1) * 128],
                                     qkv[a][:, 1, (b + j - 1) * 128:(b + j) * 128],
                                     qkv[a][:, 0, b * 128:(b + 1) * 128], start=True, stop=True) for j in js]
                P.op("pe", mm, reads=[qkvB[a][0], qkvB[a][1]], writes=[bankB[sbk]])
                lo, hi = js[0] * 128, (js[-1] + 1) * 128
                P.op("act", lambda e: e.activation(et[k][:, lo:hi], banks[sbk][:, lo:hi], AF.Exp,
                                                   scale=float(128 ** -0.5)),
                     reads=[bankB[sbk]], writes=[etB[k]])
                P.op("dve", lambda e: e.tensor_tensor(pt[k][:, lo:hi], et[k][:, lo:hi], eb[a][:, lo:hi], ALU.mult),
                     reads=[etB[k], ebB[a]], writes=[ptB[k]])
                cnt["blk"] += 1
                return (u, b, k, js)

            def pv_block(st):
                u, b, k, js = st
                a = u % 2
                g = u % 3
                dil = DILS[g]
                Lc = T // dil
                obk = OL[k]

                def mm(e):
                    r = []
                    for n, j in enumerate(js):
                        r.append(e.matmul(banks[obk][:, 0:128], Vt[a][:, b + j - 1, :], pt[k][:, j * 128:(j + 1) * 128],
                                          start=(n == 0), stop=(n == len(js) - 1)))
                    for n, j in enumerate(js):
                        r.append(e.matmul(banks[obk][:, 128:256], ones_bf, pt[k][:, j * 128:(j + 1) * 128],
                                          start=(n == 0), stop=(n == len(js) - 1)))
                    return r
                P.op("pe", mm, reads=[VtB[a], ptB[k], constB], writes=[bankB[obk]])
                cls = (128 * b) // Lc
                i0 = (128 * b) % Lc
                hp = (u // 3) % 2
                accO, accL = accO2[hp], accL2[hp]
                if dil == 1:
                    dO = accO[:, i0:i0 + 128]
                    dL = accL[:, i0:i0 + 128]
                    sO = banks[obk][:, 0:128]
                    sL = banks[obk][:, 128:256]
                else:
                    dO = accO.rearrange("p (i c) -> p i c", c=dil)[:, i0:i0 + 128, cls:cls + 1]
                    dL = accL.rearrange("p (i c) -> p i c", c=dil)[:, i0:i0 + 128, cls:cls + 1]
                    sO = banks[obk][:, 0:128].rearrange("p (i c) -> p i c", c=1)
                    sL = banks[obk][:, 128:256].rearrange("p (i c) -> p i c", c=1)
                if g == 0:
                    P.op("act", lambda e: [e.activation(dO, sO, AF.Copy), e.activation(dL, sL, AF.Copy)],
                         reads=[bankB[obk]], writes=[accOB[hp], accLB[hp]])
                else:
                    P.op("dve", lambda e: [e.tensor_tensor(dO, sO, dO, ALU.add), e.tensor_tensor(dL, sL, dL, ALU.add)],
                         reads=[bankB[obk]], writes=[accOB[hp], accLB[hp]])

            deferred = []

            def finish_head(h):
                k = cnt["ao"] % 2
                cnt["ao"] += 1
                hp = h % 2
                accO, accL = accO2[hp], accL2[hp]
                NP = 4
                W = T // NP
                for p in range(NP):
                    lo, hi = p * W, (p + 1) * W
                    deferred.append(lambda lo=lo, hi=hi: P.op(
                        "act", lambda e: e.activation(accL[:, lo:hi], accL[:, lo:hi], AF.Ln), writes=[accLB[hp]]))
                    deferred.append(lambda lo=lo, hi=hi: P.op(
                        "act", lambda e: e.activation(accL[:, lo:hi], accL[:, lo:hi], AF.Exp, scale=-1.0),
                        writes=[accLB[hp]]))
                    deferred.append(lambda lo=lo, hi=hi: P.op(
                        "dve", lambda e: e.tensor_tensor(ao[k][:, lo:hi], accO[:, lo:hi], accL[:, lo:hi], ALU.mult),
                        reads=[accOB[hp], accLB[hp]], writes=[aoB[k]]))
                dst = att_d[s * D + h * 128:s * D + (h + 1) * 128, :]
                deferred.append(lambda: P.op("sp", lambda e: e.dma_start(out=dst, in_=ao[k]), reads=[aoB[k]],
                                             writes=[attB[s][h]], dma=aoD[k]))

            load_unit_w(0)
            for rec in pj_groups(0) + tr_groups(0):
                rec()
            for u in range(NU):
                if u + 1 < NU:
                    load_unit_w(u + 1)
                    nxt = pj_groups(u + 1) + tr_groups(u + 1)
                else:
                    nxt = []
                prev = None
                for b in range(NB):
                    stt = s_block(u, b)
                    if nxt:
                        nxt.pop(0)()
                    if deferred:
                        deferred.pop(0)()
                    if prev is not None:
                        pv_block(prev)
                    prev = stt
                pv_block(prev)
                while nxt:
                    nxt.pop(0)()
                if u % 3 == 2:
                    finish_head(u // 3)
            while deferred:
                deferred.pop(0)()

        setup()
        for s in range(NSEQ):
            stats_prologue(s)
            ffn_chain(s, [(0, 0, xin_d)])
            for l in range(L):
                mixer(s, l, xs_d)
                items = [(l, 1, xs_d)]
                if l + 1 < L:
                    items.append((l + 1, 0, xs_d))
                ffn_chain(s, items, wout_l=l)
            final_norm(s)
        P.barrier()

        P.finalize(sem_pool)
        with nc.Block() as block:
            @block.tensor
            def _(e):
                P.emit("pe", e)

            @block.scalar
            def _(e):
                P.emit("act", e)

            @block.vector
            def _(e):
                P.emit("dve", e)

            @block.gpsimd
            def _(e):
                P.emit("pool", e)

            @block.sync
            def _(e):
                P.emit("sp", e)
    return nc


def _layout_weights(cfg, norm_g, ffn_gate, ffn_up, ffn_down, w_in, w_pool, pool_scale, w_out, rel_bias, final_g):
    C = cfg
    L, DC, FC, NH, D, F = C.L, C.DC, C.FC, C.NH, C.D, C.F
    A = 128 * NH
    f32 = np.float32
    g_ = np.asarray(ffn_gate, f32).reshape(L, 2, DC, 128, FC, 128)
    u_ = np.asarray(ffn_up, f32).reshape(L, 2, DC, 128, FC, 128)
    wgu = np.stack([g_, u_], axis=0)
    wgu = np.ascontiguousarray(wgu.transpose(1, 2, 5, 4, 0, 3, 6)).reshape(L * 2 * FC * 128, 2 * DC * 128)
    wd = np.asarray(ffn_down, f32).reshape(L, 2, FC, 128, DC, 128)
    wd = np.ascontiguousarray(wd.transpose(0, 1, 4, 3, 2, 5)).reshape(L * 2 * DC * 128, FC * 128)
    w_in = np.asarray(w_in, f32)
    wq = w_in[:, :, :9 * A].reshape(L, DC, 128, 3, 3, NH, 128)
    wq = np.ascontiguousarray(wq.transpose(0, 5, 3, 2, 1, 4, 6)).reshape(L * NH * 3 * 128, DC * 3 * 128)
    wpu = w_in[:, :, 9 * A:].reshape(L, DC, 128, 4, 128)
    wpu = np.ascontiguousarray(wpu.transpose(0, 3, 2, 1, 4)).reshape(L * 4 * 128, DC * 128)
    wo = np.asarray(w_out, f32).reshape(L, DC, 128, DC, 128)
    wo = np.ascontiguousarray(wo.transpose(0, 3, 2, 1, 4)).reshape(L * DC * 128, DC * 128)
    wpool = np.ascontiguousarray(np.asarray(w_pool, f32).transpose(0, 2, 1, 3)).reshape(L * 128, 4 * 128)
    gvec = np.ascontiguousarray(np.asarray(norm_g, f32).reshape(L, 3, DC, 128).transpose(3, 0, 1, 2)).reshape(128, L * 3 * DC)
    fg = np.ascontiguousarray(np.asarray(final_g, f32).reshape(DC, 128).T)
    pscale = np.ascontiguousarray(np.asarray(pool_scale, f32).reshape(L, 4, 128).transpose(2, 0, 1)).reshape(128, L * 4)
    return {
        "wgu": wgu, "wd": wd, "wqkv": wq, "wpu": wpu, "wo": wo, "wpool": wpool,
        "gvec": gvec, "fg": fg, "pscale": pscale, "rb": np.ascontiguousarray(np.asarray(rel_bias, f32)),
        "sel": _sel_const().reshape(3 * SELR, 128 * NSLOT),
        "invc": _invc_const(C.T).reshape(4 * 128, C.T),
        "ident": np.eye(128, dtype=f32),
    }


def run(cfg, seqs_per_core, weights, n_cores, trace=False):
    nc = build_program(cfg)
    shared = _layout_weights(cfg, **weights)
    in_maps = []
    for c in range(n_cores):
        xin = np.concatenate([np.ascontiguousarray(np.asarray(x, np.float32).T) for x in seqs_per_core[c]], axis=0)
        m = dict(shared)
        m["xin"] = xin
        in_maps.append(m)
    res = run_bass_kernel_spmd(nc, in_maps, core_ids=list(range(n_cores)), trace=trace)
    outs = []
    for c in range(n_cores):
        yT = res.results[c]["yT"].reshape(cfg.NSEQ, cfg.D, cfg.T)
        outs.append([np.ascontiguousarray(yT[s].T) for s in range(cfg.NSEQ)])
    return outs, res


def kernel(x_prompt, x_sample, norm_g, ffn_gate, ffn_up, ffn_down, w_in, w_pool, pool_scale,
           w_out, rel_bias, final_g):
    cfg = Cfg()
    x_prompt = np.asarray(x_prompt)
    x_sample = np.asarray(x_sample)
    seqs = [[x_prompt[2 * c], x_prompt[2 * c + 1], x_sample[c]] for c in range(8)]
    weights = dict(norm_g=norm_g, ffn_gate=ffn_gate, ffn_up=ffn_up, ffn_down=ffn_down, w_in=w_in,
                   w_pool=w_pool, pool_scale=pool_scale, w_out=w_out, rel_bias=rel_bias, final_g=final_g)
    outs, _ = run(cfg, seqs, weights, 8)
    y_prompt = np.empty(x_prompt.shape, np.float32)
    y_sample = np.empty(x_sample.shape, np.float32)
    for c in range(8):
        y_prompt[2 * c] = outs[c][0]
        y_prompt[2 * c + 1] = outs[c][1]
        y_sample[c] = outs[c][2]
    return (y_prompt, y_sample)
```

```python
import math
from contextlib import ExitStack

import numpy as np
import concourse.bass as bass
import concourse.mybir as mybir
from concourse.bass_utils import run_bass_kernel_spmd

F32 = mybir.dt.float32
BF16 = mybir.dt.bfloat16
ALU = mybir.AluOpType
AF = mybir.ActivationFunctionType

EPS = 1e-6
N_BUCKETS = 32
MAX_DISTANCE = 1024
DILS = (1, 4, 16)
POOL_H = (1, 2, 4, 8)
SEM_LIMIT = 24000
NSLOT = 3 * 128
SELR = 48


class Cfg:
    def __init__(self, NH=12, F=5632, L=4, NSEQ=3, T=2048, TT=1024):
        self.NH = NH
        self.DC = NH + 4
        self.D = 128 * self.DC
        self.F = F
        self.FC = F // 128
        self.L = L
        self.NSEQ = NSEQ
        self.T = T
        self.TT = TT
        self.NT = T // TT
        self.NU = NH * 3


class Buf:
    __slots__ = ("name", "writers", "readers")

    def __init__(self, name):
        self.name = name
        self.writers = []
        self.readers = []


class DmaSem:
    __slots__ = ("h", "count", "last")

    def __init__(self, h):
        self.h = h
        self.count = 0
        self.last = None


class Op:
    __slots__ = ("eng", "fn", "deps", "dma", "waited", "tok")

    def __init__(self, eng, fn, dma):
        self.eng = eng
        self.fn = fn
        self.dma = dma
        self.deps = ()
        self.waited = False
        self.tok = None


ENGS = ("pe", "act", "dve", "pool", "sp")
COMPUTE = ("pe", "act", "dve")


class Prog:
    def __init__(self):
        self.ops = {e: [] for e in ENGS}
        self.dma_sems = []

    def op(self, eng, fn, reads=(), writes=(), dma=None, extra=()):
        o = Op(eng, fn, dma)
        deps = set(extra)
        for b in reads:
            deps.update(b.writers)
        for b in writes:
            deps.update(b.writers)
            deps.update(b.readers)
        if eng == "pe":
            deps = {d for d in deps if d.eng != "pe"}
        o.deps = deps
        for d in deps:
            d.waited = True
        for b in reads:
            b.readers.append(o)
        for b in writes:
            b.writers = [o]
            b.readers = []
        if dma is not None:
            dma.last = o
            o.waited = True
            dma.count += 16
            o.tok = (dma.h, dma.count, 16)
        self.ops[eng].append(o)
        return o

    def barrier(self):
        lasts = []
        for e in COMPUTE:
            if self.ops[e]:
                for o in reversed(self.ops[e]):
                    if o.fn is not None:
                        lasts.append(o)
                        break
        for s in self.dma_sems:
            if s.last is not None:
                lasts.append(s.last)
        for e in ENGS:
            o = Op(e, None, None)
            o.deps = set(lasts)
            for d in lasts:
                d.waited = True
            self.ops[e].append(o)

    def finalize(self, sem_pool):
        for e in COMPUTE:
            cur = sem_pool.pop()
            cnt = 0
            for o in self.ops[e]:
                if o.fn is None or not o.waited:
                    continue
                if cnt >= SEM_LIMIT:
                    cur = sem_pool.pop()
                    cnt = 0
                cnt += 1
                o.tok = (cur, cnt, 1)
        for e in ("pool", "sp"):
            for o in self.ops[e]:
                assert o.fn is None or o.tok is not None, "DMA op without semaphore"

    def emit(self, eng, e):
        known = {}
        for o in self.ops[eng]:
            need = {}
            for d in o.deps:
                h, v, _ = d.tok
                k = h.num
                if need.get(k, (None, 0))[1] < v:
                    need[k] = (h, v)
            for k, (h, v) in need.items():
                if known.get(k, 0) >= v:
                    continue
                known[k] = v
                e.wait_ge(h, v)
            if o.fn is None:
                continue
            r = o.fn(e)
            last = r[-1] if isinstance(r, (list, tuple)) else r
            if o.tok is not None:
                last.then_inc(o.tok[0], o.tok[2])


class Arena:
    def __init__(self, tensor, nbytes):
        self.t = tensor
        self.n = nbytes
        self.off = 0
        self.mark = 0

    def alloc(self, cols, dtype):
        sz = 4 if dtype == F32 else 2
        nb = (cols * sz + 31) // 32 * 32
        assert self.off + nb <= self.n, f"arena overflow {self.off}+{nb}>{self.n}"
        a = self.t[:, self.off // 4:(self.off + nb) // 4]
        self.off += nb
        if dtype != F32:
            a = a.bitcast(dtype)
        return a[:, 0:cols]

    def set_mark(self):
        self.mark = self.off

    def reset(self):
        self.off = self.mark


def _t5_bucket_np(rel):
    half = N_BUCKETS // 2
    max_exact = half // 2
    ret = np.where(rel > 0, half, 0)
    n = np.abs(rel)
    nf = np.maximum(n, 1).astype(np.float32)
    large = max_exact + (np.log(nf / np.float32(max_exact)) / np.float32(math.log(MAX_DISTANCE / max_exact))
                         * np.float32(half - max_exact)).astype(np.int32)
    large = np.minimum(large, half - 1)
    return ret + np.where(n < max_exact, n, large)


def _sel_const():
    k = np.arange(128)[:, None, None]
    j = np.arange(3)[None, :, None]
    q = np.arange(128)[None, None, :]
    delta = k + 128 * (j - 1) - q
    out = np.zeros((3, SELR, 128 * NSLOT), np.float32)
    for g, dil in enumerate(DILS):
        b = _t5_bucket_np((delta * dil).astype(np.int32))
        b = np.where(np.abs(delta) <= 64, b, 32).reshape(-1)
        out[g, b, np.arange(b.size)] = 1.0
    return out


def _invc_const(T):
    pos = np.arange(T)
    out = np.zeros((4, 128, T), np.float32)
    for g, h in enumerate(POOL_H):
        lo = np.maximum(pos - h, 0)
        hi = np.minimum(pos + h + 1, T)
        out[g] = (np.float32(1.0) / (hi - lo).astype(np.float32))[None, :]
    return out


def build_program(cfg):
    C = cfg
    D, DC, F, FC, L, T, TT, NT, NH, NU, NSEQ = C.D, C.DC, C.F, C.FC, C.L, C.T, C.TT, C.NT, C.NH, C.NU, C.NSEQ
    NQ = T // 512
    NB = T // 128

    nc = bass.Bass("TRN2", target_bir_lowering=False)

    def din(name, shape, dt=F32):
        return nc.dram_tensor(name, list(shape), dt, kind="ExternalInput").ap()

    xin_d = din("xin", [NSEQ * D, T])
    wgu_d = din("wgu", [L * 2 * FC * 128, 2 * DC * 128])
    wd_d = din("wd", [L * 2 * DC * 128, FC * 128])
    wqkv_d = din("wqkv", [L * NU * 128, DC * 3 * 128])
    wpu_d = din("wpu", [L * 4 * 128, DC * 128])
    wo_d = din("wo", [L * DC * 128, DC * 128])
    wpool_d = din("wpool", [L * 128, 4 * 128])
    gvec_d = din("gvec", [128, L * 3 * DC])
    fg_d = din("fg", [128, DC])
    pscale_d = din("pscale", [128, L * 4])
    rb_d = din("rb", [32, 3 * NH])
    sel_d = din("sel", [3 * SELR, 128 * NSLOT])
    invc_d = din("invc", [4 * 128, T])
    ident_d = din("ident", [128, 128])
    yT_d = nc.dram_tensor("yT", [NSEQ * D, T], F32, kind="ExternalOutput").ap()
    xs_d = nc.dram_tensor("xs", [NSEQ * D, T], F32, kind="Internal").ap()
    att_d = nc.dram_tensor("attT", [NSEQ * D, T], BF16, kind="Internal").ap()
    expb_d = nc.dram_tensor("expb", [NU, 128 * NSLOT], F32, kind="Internal").ap()

    P = Prog()
    es = ExitStack()
    with es:
        arena_t = es.enter_context(nc.sbuf_tensor("arena", [128, 212000 // 4], F32))
        AR = Arena(arena_t, 212000)
        banks = [es.enter_context(nc.psum_tensor(f"bank{i}", [128, 512], F32)) for i in range(8)]
        bankB = [Buf(f"bank{i}") for i in range(8)]
        sem_pool = [es.enter_context(nc.semaphore(f"s{i}")) for i in range(100)]

        free_ds = []
        inuse_ds = []

        def dsem():
            if free_ds:
                s = free_ds.pop(0)
            else:
                s = DmaSem(sem_pool.pop())
                P.dma_sems.append(s)
            inuse_ds.append(s)
            return s

        def phase_begin():
            P.barrier()
            while inuse_ds:
                s = inuse_ds.pop(0)
                if s.count > 12000:
                    s.h = sem_pool.pop()
                    s.count = 0
                    s.last = None
                free_ds.append(s)
            AR.reset()

        ones_bf = AR.alloc(128, BF16)
        ident_bf = AR.alloc(128, BF16)
        gvec = AR.alloc(L * 3 * DC, F32)
        fg = AR.alloc(DC, F32)
        pscale = AR.alloc(L * 4, F32)
        rstd = AR.alloc(T, F32)
        epsc = AR.alloc(8, F32)
        constB = Buf("const")
        rstdB = [Buf(f"rstd{t}") for t in range(NT)]
        AR.set_mark()

        xB = [[[Buf(f"x{s}_{t}_{c}") for c in range(DC)] for t in range(NT)] for s in range(NSEQ)]
        attB = [[Buf(f"att{s}_{c}") for c in range(DC)] for s in range(NSEQ)]
        expbB = [Buf(f"expb{g}") for g in range(3)]
        outB = Buf("out")

        def setup():
            cs = dsem()
            P.op("dve", lambda e: e.memset(ones_bf, 1.0), writes=[constB])
            P.op("dve", lambda e: e.memset(epsc, EPS), writes=[constB])
            P.op("pool", lambda e: e.dma_start(out=ident_bf, in_=ident_d[:, :]), writes=[constB], dma=cs)
            cs2 = dsem()
            P.op("sp", lambda e: e.dma_start(out=gvec, in_=gvec_d[:, :]), writes=[constB], dma=cs2)
            cs3 = dsem()
            P.op("sp", lambda e: e.dma_start(out=fg, in_=fg_d[:, :]), writes=[constB], dma=cs3)
            cs4 = dsem()
            P.op("sp", lambda e: e.dma_start(out=pscale, in_=pscale_d[:, :]), writes=[constB], dma=cs4)
            rb = AR.alloc(3 * NH, F32)
            rbB = Buf("rb")
            P.op("dve", lambda e: e.memset(rb[0:64, :], -30000.0), writes=[rbB])
            cs5 = dsem()
            P.op("sp", lambda e: e.dma_start(out=rb[0:32, :], in_=rb_d[:, :]), writes=[rbB], dma=cs5)
            PC = 1024
            NSL = 4
            selS = [AR.alloc(PC, F32) for _ in range(NSL)]
            selB = [Buf(f"sel{i}") for i in range(NSL)]
            selD = [dsem() for _ in range(NSL)]
            stg = [AR.alloc(PC, F32) for _ in range(NSL)]
            stgB = [Buf(f"stg{i}") for i in range(NSL)]
            stgD = [dsem() for _ in range(NSL)]
            it = 0
            for g in range(3):
                for pc in range(128 * NSLOT // PC):
                    sl = it % NSL
                    it += 1
                    P.op("sp", lambda e, g=g, pc=pc, sl=sl: e.dma_start(
                        out=selS[sl][0:SELR, :], in_=sel_d[g * SELR:(g + 1) * SELR, pc * PC:(pc + 1) * PC]),
                        writes=[selB[sl]], dma=selD[sl])
                    bk = [2 * sl + i for i in range(2)]

                    def mm(e, g=g, sl=sl, bk=bk):
                        r = []
                        for i in range(2):
                            r.append(e.matmul(banks[bk[i]][0:NH, :], rb[0:SELR, g * NH:(g + 1) * NH],
                                              selS[sl][0:SELR, i * 512:(i + 1) * 512], start=True, stop=True))
                        return r
                    P.op("pe", mm, reads=[rbB, selB[sl]], writes=[bankB[b] for b in bk])

                    def ex(e, sl=sl, bk=bk):
                        r = []
                        for i in range(2):
                            r.append(e.activation(stg[sl][0:NH, i * 512:(i + 1) * 512], banks[bk[i]][0:NH, :], AF.Exp))
                        return r
                    P.op("act", ex, reads=[bankB[b] for b in bk], writes=[stgB[sl]])
                    dst = expb_d.rearrange("(h g) n -> h g n", g=3)[:, g, pc * PC:(pc + 1) * PC]
                    P.op("pool", lambda e, sl=sl, dst=dst: e.dma_start(out=dst, in_=stg[sl][0:NH, :]),
                         reads=[stgB[sl]], writes=[expbB[g]], dma=stgD[sl])

        class XIO:
            def __init__(self, nin=3, nst=2, nsq=2):
                self.xin = [AR.alloc(TT, F32) for _ in range(nin)]
                self.xinB = [Buf(f"xin{i}") for i in range(nin)]
                self.xinD = [dsem() for _ in range(nin)]
                self.st = [AR.alloc(TT, F32) for _ in range(nst)]
                self.stB = [Buf(f"st{i}") for i in range(nst)]
                self.stD = [dsem() for _ in range(nst)]
                self.sq = [AR.alloc(TT, BF16) for _ in range(nsq)]
                self.sqB = [Buf(f"sq{i}") for i in range(nsq)]
                self.i_in = 0
                self.i_st = 0
                self.i_sq = 0

            def load(self, src_d, s, t, c):
                sl = self.i_in % len(self.xin)
                self.i_in += 1
                src = src_d[s * D + c * 128:s * D + (c + 1) * 128, t * TT:(t + 1) * TT]
                self.last_load = P.op("sp", lambda e: e.dma_start(out=self.xin[sl], in_=src),
                                      reads=[xB[s][t][c]], writes=[self.xinB[sl]], dma=self.xinD[sl])
                return sl

            def stage(self):
                sl = self.i_st % len(self.st)
                self.i_st += 1
                return sl

            def sqslot(self):
                sl = self.i_sq % len(self.sq)
                self.i_sq += 1
                return sl

        def stats_mm(io, sq, c, ST, STB):
            def f(e):
                r = []
                for hf in range(TT // 512):
                    r.append(e.matmul(banks[ST[hf]][:, :], ones_bf, io.sq[sq][:, hf * 512:(hf + 1) * 512],
                                      start=(c == 0), stop=(c == DC - 1)))
                return r
            P.op("pe", f, reads=[io.sqB[sq], constB], writes=STB)

        def stats_finish(io, t, ST, STB, tmpf, tmpfB):
            def f(e):
                r = []
                for hf in range(TT // 512):
                    r.append(e.activation(tmpf[:, hf * 512:(hf + 1) * 512], banks[ST[hf]][:, :], AF.Ln,
                                          bias=epsc[:, 0:1], scale=1.0 / D))
                return r
            P.op("act", f, reads=STB + [constB], writes=[tmpfB])
            P.op("act", lambda e: e.activation(rstd[:, t * TT:(t + 1) * TT], tmpf, AF.Exp, scale=-0.5),
                 reads=[tmpfB], writes=[rstdB[t]])

        def stats_prologue(s):
            phase_begin()
            io = XIO(nin=8)
            tmpf = AR.alloc(TT, F32)
            tmpfB = Buf("tmpf")
            ST = [4, 5]
            STB = [bankB[4], bankB[5]]
            for t in range(NT):
                for c in range(DC):
                    sl = io.load(xin_d, s, t, c)
                    q = io.sqslot()
                    P.op("act", lambda e, sl=sl, q=q: e.activation(io.sq[q], io.xin[sl], AF.Square),
                         reads=[io.xinB[sl]], writes=[io.sqB[q]])
                    stats_mm(io, q, c, ST, STB)
                stats_finish(io, t, ST, STB, tmpf, tmpfB)

        def final_norm(s):
            phase_begin()
            io = XIO(nin=6, nst=4, nsq=0)
            for t in range(NT):
                for c in range(DC):
                    sl = io.load(xs_d, s, t, c)
                    st = io.stage()
                    P.op("dve", lambda e, sl=sl, st=st, c=c, t=t: e.scalar_tensor_tensor(
                        io.st[st], io.xin[sl], fg[:, c:c + 1], rstd[:, t * TT:(t + 1) * TT], ALU.mult, ALU.mult),
                        reads=[io.xinB[sl], rstdB[t], constB], writes=[io.stB[st]])
                    dst = yT_d[s * D + c * 128:s * D + (c + 1) * 128, t * TT:(t + 1) * TT]
                    P.op("sp", lambda e, st=st, dst=dst: e.dma_start(out=dst, in_=io.st[st]),
                         reads=[io.stB[st]], writes=[outB], dma=io.stD[st])

        def make_h(io, src_d, s, t, gcol, hT, hTB, toff):
            for c in range(DC):
                sl = io.load(src_d, s, t, c)
                P.op("dve", lambda e, sl=sl, c=c: e.scalar_tensor_tensor(
                    hT[:, c, toff:toff + TT], io.xin[sl], gvec[:, gcol + c:gcol + c + 1],
                    rstd[:, t * TT:(t + 1) * TT], ALU.mult, ALU.mult),
                    reads=[io.xinB[sl], rstdB[t], constB], writes=[hTB[c]])

        def proj(io, s, t, src_d, actT, actB, KC, w_d, wrow0, wS, wSB, wSD, scale, tmpf, tmpfB):
            Y = [[0, 1], [2, 3]]
            ST = [4, 5]
            STB = [bankB[4], bankB[5]]
            pend = None
            for dc in range(DC):
                ws = dc % 2
                par = dc % 2
                wsrc = w_d[wrow0 + dc * 128:wrow0 + (dc + 1) * 128, :]
                P.op("pool", lambda e, ws=ws, wsrc=wsrc: e.dma_start(out=wS[ws][:, 0:KC * 128], in_=wsrc),
                     writes=[wSB[ws]], dma=wSD[ws])
                sl = io.load(src_d, s, t, dc)
                yb = Y[par]

                def mm(e, ws=ws, yb=yb):
                    r = []
                    wv = wS[ws][:, 0:KC * 128].rearrange("p (k d) -> p k d", d=128)
                    for hf in range(TT // 512):
                        for k in range(KC):
                            r.append(e.matmul(banks[yb[hf]][:, :], wv[:, k, :], actT[:, k, hf * 512:(hf + 1) * 512],
                                              start=(k == 0), stop=(k == KC - 1)))
                    return r
                P.op("pe", mm, reads=[wSB[ws]] + list(actB), writes=[bankB[b] for b in yb])
                if pend is not None:
                    stats_mm(io, pend[0], pend[1], ST, STB)
                st = io.stage()

                def ep(e, sl=sl, st=st, yb=yb):
                    r = []
                    for hf in range(TT // 512):
                        r.append(e.scalar_tensor_tensor(io.st[st][:, hf * 512:(hf + 1) * 512], banks[yb[hf]][:, :],
                                                        float(scale), io.xin[sl][:, hf * 512:(hf + 1) * 512],
                                                        ALU.mult, ALU.add))
                    return r
                P.op("dve", ep, reads=[bankB[b] for b in yb] + [io.xinB[sl]], writes=[io.stB[st]])
                dst = xs_d[s * D + dc * 128:s * D + (dc + 1) * 128, t * TT:(t + 1) * TT]
                P.op("sp", lambda e, st=st, dst=dst: e.dma_start(out=dst, in_=io.st[st]),
                     reads=[io.stB[st]], writes=[xB[s][t][dc]], dma=io.stD[st])
                q = io.sqslot()
                P.op("act", lambda e, st=st, q=q: e.activation(io.sq[q], io.st[st], AF.Square),
                     reads=[io.stB[st]], writes=[io.sqB[q]])
                pend = (q, dc)
            stats_mm(io, pend[0], pend[1], ST, STB)
            stats_finish(io, t, ST, STB, tmpf, tmpfB)

        def ffn_chain(s, items, wout_l=None):
            phase_begin()
            io = XIO()
            hT = AR.alloc(DC * TT, BF16).rearrange("p (c t) -> p c t", t=TT)
            hTB = [Buf(f"hT{c}") for c in range(DC)]
            aT = AR.alloc(FC * TT, BF16).rearrange("p (c t) -> p c t", t=TT)
            aTB = [Buf(f"aT{c}") for c in range(FC)]
            wgu = [AR.alloc(2 * DC * 128, BF16) for _ in range(2)]
            wguB = [Buf("wgu0"), Buf("wgu1")]
            wguD = [dsem(), dsem()]
            wd = [AR.alloc(FC * 128, BF16) for _ in range(2)]
            wdB = [Buf("wd0"), Buf("wd1")]
            wdD = [dsem(), dsem()]
            tmp = [AR.alloc(TT, F32) for _ in range(2)]
            tmpB = [Buf("tmp0"), Buf("tmp1")]
            tmpf = AR.alloc(TT, F32)
            tmpfB = Buf("tmpf")
            GB = [[0, 1], [4, 5]]
            UB = [[2, 3], [6, 7]]
            jobs = [(l, i, src_d, t) for (l, i, src_d) in items for t in range(NT)]

            def do_h(job):
                l, i, src_d, t = job
                gcol = (l * 3 + (0 if i == 0 else 2)) * DC
                make_h(io, src_d, s, t, gcol, hT, hTB, 0)

            def do_gu(job, first_extra):
                l, i, src_d, t = job
                for fc in range(FC):
                    ws = fc % 2
                    par = fc % 2
                    row = ((l * 2 + i) * FC + fc) * 128
                    wsrc = wgu_d[row:row + 128, :]
                    P.op("pool", lambda e, ws=ws, wsrc=wsrc: e.dma_start(out=wgu[ws], in_=wsrc),
                         writes=[wguB[ws]], dma=wguD[ws], extra=(first_extra if fc == 0 else ()))
                    gb, ub = GB[par], UB[par]

                    def mm(e, ws=ws, gb=gb, ub=ub):
                        r = []
                        wv = wgu[ws].rearrange("p (j c f) -> p j c f", j=2, f=128)
                        for j, bb in ((0, gb), (1, ub)):
                            for hf in range(TT // 512):
                                for c in range(DC):
                                    r.append(e.matmul(banks[bb[hf]][:, :], wv[:, j, c, :], hT[:, c, hf * 512:(hf + 1) * 512],
                                                      start=(c == 0), stop=(c == DC - 1)))
                        return r
                    P.op("pe", mm, reads=[wguB[ws]] + hTB, writes=[bankB[b] for b in gb + ub])

                    def sil(e, par=par, gb=gb):
                        return [e.activation(tmp[par][:, hf * 512:(hf + 1) * 512], banks[gb[hf]][:, :], AF.Silu)
                                for hf in range(TT // 512)]
                    P.op("act", sil, reads=[bankB[b] for b in gb], writes=[tmpB[par]])

                    def mul(e, par=par, ub=ub, fc=fc):
                        return [e.tensor_tensor(aT[:, fc, hf * 512:(hf + 1) * 512], banks[ub[hf]][:, :],
                                                tmp[par][:, hf * 512:(hf + 1) * 512], ALU.mult)
                                for hf in range(TT // 512)]
                    P.op("dve", mul, reads=[bankB[b] for b in ub] + [tmpB[par]], writes=[aTB[fc]])

            def do_down(job):
                l, i, src_d, t = job
                proj(io, s, t, src_d, aT, aTB, FC, wd_d, (l * 2 + i) * DC * 128, wd, wdB, wdD, 0.5, tmpf, tmpfB)

            first_extra = ()
            if wout_l is not None:
                assert FC >= NT * DC
                atD = [dsem() for _ in range(NT)]
                for t in range(NT):
                    src = att_d[s * D:(s + 1) * D, t * TT:(t + 1) * TT].rearrange("(c p) t -> p c t", p=128)
                    P.op("sp", lambda e, src=src, t=t: e.dma_start(out=aT[:, t * DC:(t + 1) * DC, :], in_=src),
                         reads=attB[s], writes=aTB[t * DC:(t + 1) * DC], dma=atD[t])
                for t in range(NT):
                    proj(io, s, t, xs_d, aT[:, t * DC:(t + 1) * DC, :], aTB[t * DC:(t + 1) * DC], DC, wo_d,
                         wout_l * DC * 128, wd, wdB, wdD, 1.0, tmpf, tmpfB)
                    if t == 0:
                        do_h(jobs[0])
            else:
                do_h(jobs[0])
                first_extra = (io.last_load,)
            for n, job in enumerate(jobs):
                do_gu(job, first_extra if n == 0 else ())
                if n + 1 < len(jobs):
                    do_h(jobs[n + 1])
                do_down(job)

        def mixer(s, l, src_d):
            phase_begin()
            io = XIO(nin=5, nst=0, nsq=0)
            hT = AR.alloc(DC * T, BF16).rearrange("p (c t) -> p c t", t=T)
            hTBt = [[Buf(f"mhT{t}_{c}") for c in range(DC)] for t in range(NT)]
            hTB = [b for row in hTBt for b in row]
            ao = [AR.alloc(T, BF16) for _ in range(2)]
            aoB = [Buf("ao0"), Buf("ao1")]
            aoD = [dsem(), dsem()]
            gcol = (l * 3 + 1) * DC
            mix_first = []
            for t in range(NT):
                make_h(io, src_d, s, t, gcol, hT, hTBt[t], t * TT)
                if t == 0:
                    mix_first = [io.last_load]
            PJ = [0, 1]
            cnt = {"pj": 0, "tr": 0, "blk": 0, "ev": 0, "ao": 0}
            wq = [AR.alloc(DC * 3 * 128, BF16) for _ in range(2)]
            wqB = [Buf("wq0"), Buf("wq1")]
            wqD = [dsem(), dsem()]
            qkv = [AR.alloc(3 * T, BF16).rearrange("p (j t) -> p j t", t=T) for _ in range(2)]
            qkvB = [[Buf(f"qkv{a}_{j}") for j in range(3)] for a in range(2)]
            Vt = [AR.alloc(NB * 128, BF16).rearrange("p (k d) -> p k d", d=128) for _ in range(2)]
            VtB = [Buf("V0"), Buf("V1")]
            sub_mark = AR.off

            def pool_branch(g0):
                TP = T + 16
                Up = AR.alloc(TP, F32)
                sA = AR.alloc(TP, F32)
                sBf = AR.alloc(TP, F32)
                dT = AR.alloc(T, BF16)
                UpB, sAB, sBB, dTB = Buf("Up"), Buf("sA"), Buf("sB"), Buf("dT")
                wpu = [AR.alloc(DC * 128, BF16) for _ in range(2)]
                wpuB = [Buf("wpu0"), Buf("wpu1")]
                wpuD = [dsem(), dsem()]
                wpl = AR.alloc(4 * 128, BF16)
                wplB = Buf("wpl")
                wplD = dsem()
                ivc = AR.alloc(T, F32)
                ivcB = Buf("ivc")
                ivcD = dsem()
                P.op("dve", lambda e: e.memset(Up, 0.0), writes=[UpB])
                P.op("pool", lambda e: e.dma_start(out=wpl, in_=wpool_d[l * 128:(l + 1) * 128, :]), writes=[wplB], dma=wplD,
                     extra=mix_first)
                for pg in range(4):
                    ws = pg % 2
                    row = (l * 4 + pg) * 128
                    P.op("pool", lambda e, ws=ws, row=row: e.dma_start(out=wpu[ws], in_=wpu_d[row:row + 128, :]),
                         writes=[wpuB[ws]], dma=wpuD[ws])
                    P.op("sp", lambda e, pg=pg: e.dma_start(out=ivc, in_=invc_d[pg * 128:(pg + 1) * 128, :]),
                         writes=[ivcB], dma=ivcD)
                    for tq in range(NQ):
                        bk = PJ[cnt["pj"] % 2]
                        cnt["pj"] += 1

                        def mm(e, ws=ws, tq=tq, bk=bk):
                            wv = wpu[ws].rearrange("p (c f) -> p c f", f=128)
                            return [e.matmul(banks[bk][:, :], wv[:, c, :], hT[:, c, tq * 512:(tq + 1) * 512],
                                             start=(c == 0), stop=(c == DC - 1)) for c in range(DC)]
                        P.op("pe", mm, reads=[wpuB[ws]] + hTBt[(tq * 512) // TT], writes=[bankB[bk]])
                        P.op("act", lambda e, tq=tq, bk=bk: e.activation(Up[:, 8 + tq * 512:8 + (tq + 1) * 512],
                                                                        banks[bk][:, :], AF.Copy),
                             reads=[bankB[bk]], writes=[UpB])
                    for _ in range(3):
                        if g0:
                            g0.pop(0)()
                    hh = POOL_H[pg]
                    src, srcB = Up, UpB
                    pp = [(sA, sAB), (sBf, sBB)]
                    pi = 0
                    kk = 1
                    while kk <= hh:
                        n = TP - (2 * kk - 1)
                        dst, dstB = pp[pi]
                        pi ^= 1
                        P.op("dve", lambda e, dst=dst, src=src, n=n, kk=kk: e.tensor_tensor(
                            dst[:, 0:n], src[:, 0:n], src[:, kk:kk + n], ALU.add), reads=[srcB], writes=[dstB])
                        src, srcB = dst, dstB
                        kk *= 2
                    dst, dstB = pp[pi]
                    pi ^= 1
                    P.op("dve", lambda e, dst=dst, src=src, hh=hh: e.tensor_tensor(
                        dst[:, 0:T], src[:, 8 - hh:8 - hh + T], Up[:, 8 + hh:8 + hh + T], ALU.add),
                        reads=[srcB, UpB], writes=[dstB])
                    w_, wB_ = dst, dstB
                    dst, dstB = pp[pi]
                    P.op("dve", lambda e, dst=dst, w_=w_: e.tensor_tensor(dst[:, 0:T], w_[:, 0:T], ivc, ALU.mult),
                         reads=[wB_, ivcB], writes=[dstB])
                    P.op("dve", lambda e, dst=dst: e.tensor_tensor(dT, dst[:, 0:T], Up[:, 8:8 + T], ALU.subtract),
                         reads=[dstB, UpB], writes=[dTB])
                    k = cnt["ao"] % 2
                    cnt["ao"] += 1
                    for tq in range(NQ):
                        bk = PJ[cnt["pj"] % 2]
                        cnt["pj"] += 1
                        P.op("pe", lambda e, pg=pg, tq=tq, bk=bk: e.matmul(
                            banks[bk][:, :], wpl[:, pg * 128:(pg + 1) * 128], dT[:, tq * 512:(tq + 1) * 512],
                            start=True, stop=True), reads=[wplB, dTB], writes=[bankB[bk]])
                        P.op("act", lambda e, pg=pg, tq=tq, bk=bk, k=k: e.activation(
                            ao[k][:, tq * 512:(tq + 1) * 512], banks[bk][:, :], AF.Copy,
                            scale=pscale[:, l * 4 + pg:l * 4 + pg + 1]),
                            reads=[bankB[bk], constB], writes=[aoB[k]])
                    dst = att_d[s * D + (NH + pg) * 128:s * D + (NH + pg + 1) * 128, :]
                    P.op("sp", lambda e, k=k, dst=dst: e.dma_start(out=dst, in_=ao[k]),
                         reads=[aoB[k]], writes=[attB[s][NH + pg]], dma=aoD[k])


            TR = [2, 7]
            SB = [3, 4]
            OL = [5, 6]

            def load_unit_wq(u, extra=()):
                ws = u % 2
                row = (l * NU + u) * 128
                wsrc = wqkv_d[row:row + 128, :]
                P.op("pool", lambda e: e.dma_start(out=wq[ws], in_=wsrc), writes=[wqB[ws]], dma=wqD[ws], extra=extra)

            def load_unit_eb(u):
                ws = u % 2
                src = expb_d[u].rearrange("(k n) -> k n", n=NSLOT)
                P.op("sp", lambda e: e.dma_start(out=eb[ws], in_=src), reads=[expbB[u % 3]],
                     writes=[ebB[ws]], dma=ebD[ws])

            def load_unit_w(u):
                load_unit_wq(u)
                load_unit_eb(u)

            def pj_groups(u, act_only=False):
                a = u % 2
                g = u % 3
                dil = DILS[g]
                Lc = T // dil
                out = []
                for tq in range(NQ):
                    for j in range(3):
                        def rec(tq=tq, j=j):
                            bk = PJ[cnt["pj"] % 2]
                            cnt["pj"] += 1
                            wv = wq[a].rearrange("p (c j f) -> p c j f", j=3, f=128)

                            def mm(e):
                                return [e.matmul(banks[bk][:, :], wv[:, c, j, :], hT[:, c, tq * 512:(tq + 1) * 512],
                                                 start=(c == 0), stop=(c == DC - 1)) for c in range(DC)]
                            P.op("pe", mm, reads=[wqB[a]] + hTBt[(tq * 512) // TT], writes=[bankB[bk]])
                            ni = 512 // dil
                            i0 = tq * ni
                            if dil == 1:
                                dst = qkv[a][:, j, tq * 512:(tq + 1) * 512]
                                srcv = banks[bk][:, :]
                            else:
                                dst = qkv[a][:, j, :].rearrange("p (c i) -> p c i", c=dil)[:, :, i0:i0 + ni]
                                dst = dst.rearrange("p c i -> p i c")
                                srcv = banks[bk][:, :].rearrange("p (i c) -> p i c", c=dil)
                            if act_only or (cnt["ev"] % 2) == 0:
                                P.op("act", lambda e: e.activation(dst, srcv, AF.Copy),
                                     reads=[bankB[bk]], writes=[qkvB[a][j]])
                            else:
                                P.op("dve", lambda e: e.tensor_copy(dst, srcv),
                                     reads=[bankB[bk]], writes=[qkvB[a][j]])
                            cnt["ev"] += 1
                        out.append(rec)
                return out

            def tr_groups(u):
                a = u % 2
                return [lambda grp=grp: tr_group(a, grp) for grp in range(NB // 4)]

            def tr_group(a, grp):
                if True:
                    bk = TR[cnt["tr"] % 2]
                    cnt["tr"] += 1
                    pb = banks[bk][:, :].bitcast(BF16)

                    def tr(e, grp=grp, pb=pb):
                        return [e.transpose(pb[:, i * 128:(i + 1) * 128],
                                            qkv[a][:, 2, (grp * 4 + i) * 128:(grp * 4 + i + 1) * 128], ident_bf)
                                for i in range(4)]
                    P.op("pe", tr, reads=[qkvB[a][2], constB], writes=[bankB[bk]])
                    dst = Vt[a][:, grp * 4:(grp + 1) * 4, :]
                    srcv = pb[:, 0:512].rearrange("p (k d) -> p k d", d=128)
                    P.op("dve", lambda e, dst=dst, srcv=srcv: e.tensor_copy(dst, srcv),
                         reads=[bankB[bk]], writes=[VtB[a]])

            def valid_slots(g, b):
                bpc = (T // DILS[g]) // 128
                return [j for j in range(3) if 0 <= b + j - 1 < NB and (b + j - 1) // bpc == b // bpc]

            def s_block(u, b):
                a = u % 2
                g = u % 3
                k = cnt["blk"] % 2
                js = valid_slots(g, b)
                sbk = SB[k]

                def mm(e):
                    return [e.matmul(banks[sbk][:, j * 128:(j + 1) * 128],
                                     qkv[a][:, 1, (b + j - 1) * 128:(b + j) * 128],
                                     qkv[a][:, 0, b * 128:(b + 1) * 128], start=True, stop=True) for j in js]
                P.op("pe", mm, reads=[qkvB[a][0], qkvB[a][1]], writes=[bankB[sbk]])
                lo, hi = js[0] * 128, (js[-1] + 1) * 128
                P.op("act", lambda e: e.activation(et[k][:, lo:hi], banks[sbk][:, lo:hi], AF.Exp,
                                                   scale=float(128 ** -0.5)),
                     reads=[bankB[sbk]], writes=[etB[k]])
                P.op("dve", lambda e: e.tensor_tensor(pt[k][:, lo:hi], et[k][:, lo:hi], eb[a][:, lo:hi], ALU.mult),
                     reads=[etB[k], ebB[a]], writes=[ptB[k]])
                cnt["blk"] += 1
                return (u, b, k, js)

            def pv_block(st):
                u, b, k, js = st
                a = u % 2
                g = u % 3
                dil = DILS[g]
                Lc = T // dil
                obk = OL[k]

                def mm(e):
                    r = []
                    for n, j in enumerate(js):
                        r.append(e.matmul(banks[obk][:, 0:128], Vt[a][:, b + j - 1, :], pt[k][:, j * 128:(j + 1) * 128],
                                          start=(n == 0), stop=(n == len(js) - 1)))
                    for n, j in enumerate(js):
                        r.append(e.matmul(banks[obk][:, 128:256], ones_bf, pt[k][:, j * 128:(j + 1) * 128],
                                          start=(n == 0), stop=(n == len(js) - 1)))
                    return r
                P.op("pe", mm, reads=[VtB[a], ptB[k], constB], writes=[bankB[obk]])
                cls = (128 * b) // Lc
                i0 = (128 * b) % Lc
                hp = (u // 3) % 2
                accO, accL = accO2[hp], accL2[hp]
                if dil == 1:
                    dO = accO[:, i0:i0 + 128]
                    dL = accL[:, i0:i0 + 128]
                    sO = banks[obk][:, 0:128]
                    sL = banks[obk][:, 128:256]
                else:
                    dO = accO.rearrange("p (i c) -> p i c", c=dil)[:, i0:i0 + 128, cls:cls + 1]
                    dL = accL.rearrange("p (i c) -> p i c", c=dil)[:, i0:i0 + 128, cls:cls + 1]
                    sO = banks[obk][:, 0:128].rearrange("p (i c) -> p i c", c=1)
                    sL = banks[obk][:, 128:256].rearrange("p (i c) -> p i c", c=1)
                if g == 0:
                    P.op("act", lambda e: [e.activation(dO, sO, AF.Copy), e.activation(dL, sL, AF.Copy)],
                         reads=[bankB[obk]], writes=[accOB[hp], accLB[hp]])
                else:
                    P.op("dve", lambda e: [e.tensor_tensor(dO, sO, dO, ALU.add), e.tensor_tensor(dL, sL, dL, ALU.add)],
                         reads=[bankB[obk]], writes=[accOB[hp], accLB[hp]])

            deferred = []

            def finish_head(h):
                k = cnt["ao"] % 2
                cnt["ao"] += 1
                hp = h % 2
                accO, accL = accO2[hp], accL2[hp]
                NP = 4
                W = T // NP
                for p in range(NP):
                    lo, hi = p * W, (p + 1) * W
                    deferred.append(lambda lo=lo, hi=hi: P.op(
                        "act", lambda e: e.activation(accL[:, lo:hi], accL[:, lo:hi], AF.Ln), writes=[accLB[hp]]))
                    deferred.append(lambda lo=lo, hi=hi: P.op(
                        "act", lambda e: e.activation(accL[:, lo:hi], accL[:, lo:hi], AF.Exp, scale=-1.0),
                        writes=[accLB[hp]]))
                    deferred.append(lambda lo=lo, hi=hi: P.op(
                        "dve", lambda e: e.tensor_tensor(ao[k][:, lo:hi], accO[:, lo:hi], accL[:, lo:hi], ALU.mult),
                        reads=[accOB[hp], accLB[hp]], writes=[aoB[k]]))
                dst = att_d[s * D + h * 128:s * D + (h + 1) * 128, :]
                deferred.append(lambda: P.op("sp", lambda e: e.dma_start(out=dst, in_=ao[k]), reads=[aoB[k]],
                                             writes=[attB[s][h]], dma=aoD[k]))

            load_unit_wq(0, extra=mix_first)
            g0 = pj_groups(0, act_only=True)
            pool_branch(g0)
            P.barrier()
            AR.off = sub_mark
            eb = [AR.alloc(NSLOT, F32) for _ in range(2)]
            ebB = [Buf("eb0"), Buf("eb1")]
            ebD = [dsem(), dsem()]
            et = [AR.alloc(NSLOT, F32) for _ in range(2)]
            etB = [Buf("et0"), Buf("et1")]
            pt = [AR.alloc(NSLOT, BF16) for _ in range(2)]
            ptB = [Buf("pt0"), Buf("pt1")]
            accO2 = [AR.alloc(T, F32) for _ in range(2)]
            accL2 = [AR.alloc(T, F32) for _ in range(2)]
            accOB = [Buf(f"accO{b}") for b in range(2)]
            accLB = [Buf(f"accL{b}") for b in range(2)]

            load_unit_eb(0)
            for rec in g0 + tr_groups(0):
                rec()
            for u in range(NU):
                if u + 1 < NU:
                    load_unit_w(u + 1)
                    nxt = pj_groups(u + 1) + tr_groups(u + 1)
                else:
                    nxt = []
                prev = None
                for b in range(NB):
                    stt = s_block(u, b)
                    if nxt:
                        nxt.pop(0)()
                    if deferred:
                        deferred.pop(0)()
                    if prev is not None:
                        pv_block(prev)
                    prev = stt
                pv_block(prev)
                while nxt:
                    nxt.pop(0)()
                if u % 3 == 2:
                    finish_head(u // 3)
            while deferred:
                deferred.pop(0)()

        setup()
        for s in range(NSEQ):
            stats_prologue(s)
            ffn_chain(s, [(0, 0, xin_d)])
            for l in range(L):
                mixer(s, l, xs_d)
                items = [(l, 1, xs_d)]
                if l + 1 < L:
                    items.append((l + 1, 0, xs_d))
                ffn_chain(s, items, wout_l=l)
            final_norm(s)
        P.barrier()

        P.finalize(sem_pool)
        with nc.Block() as block:
            @block.tensor
            def _(e):
                P.emit("pe", e)

            @block.scalar
            def _(e):
                P.emit("act", e)

            @block.vector
            def _(e):
                P.emit("dve", e)

            @block.gpsimd
            def _(e):
                P.emit("pool", e)

            @block.sync
            def _(e):
                P.emit("sp", e)
    return nc


def _layout_weights(cfg, norm_g, ffn_gate, ffn_up, ffn_down, w_in, w_pool, pool_scale, w_out, rel_bias, final_g):
    C = cfg
    L, DC, FC, NH, D, F = C.L, C.DC, C.FC, C.NH, C.D, C.F
    A = 128 * NH
    f32 = np.float32
    g_ = np.asarray(ffn_gate, f32).reshape(L, 2, DC, 128, FC, 128)
    u_ = np.asarray(ffn_up, f32).reshape(L, 2, DC, 128, FC, 128)
    wgu = np.stack([g_, u_], axis=0)
    wgu = np.ascontiguousarray(wgu.transpose(1, 2, 5, 4, 0, 3, 6)).reshape(L * 2 * FC * 128, 2 * DC * 128)
    wd = np.asarray(ffn_down, f32).reshape(L, 2, FC, 128, DC, 128)
    wd = np.ascontiguousarray(wd.transpose(0, 1, 4, 3, 2, 5)).reshape(L * 2 * DC * 128, FC * 128)
    w_in = np.asarray(w_in, f32)
    wq = w_in[:, :, :9 * A].reshape(L, DC, 128, 3, 3, NH, 128)
    wq = np.ascontiguousarray(wq.transpose(0, 5, 3, 2, 1, 4, 6)).reshape(L * NH * 3 * 128, DC * 3 * 128)
    wpu = w_in[:, :, 9 * A:].reshape(L, DC, 128, 4, 128)
    wpu = np.ascontiguousarray(wpu.transpose(0, 3, 2, 1, 4)).reshape(L * 4 * 128, DC * 128)
    wo = np.asarray(w_out, f32).reshape(L, DC, 128, DC, 128)
    wo = np.ascontiguousarray(wo.transpose(0, 3, 2, 1, 4)).reshape(L * DC * 128, DC * 128)
    wpool = np.ascontiguousarray(np.asarray(w_pool, f32).transpose(0, 2, 1, 3)).reshape(L * 128, 4 * 128)
    gvec = np.ascontiguousarray(np.asarray(norm_g, f32).reshape(L, 3, DC, 128).transpose(3, 0, 1, 2)).reshape(128, L * 3 * DC)
    fg = np.ascontiguousarray(np.asarray(final_g, f32).reshape(DC, 128).T)
    pscale = np.ascontiguousarray(np.asarray(pool_scale, f32).reshape(L, 4, 128).transpose(2, 0, 1)).reshape(128, L * 4)
    return {
        "wgu": wgu, "wd": wd, "wqkv": wq, "wpu": wpu, "wo": wo, "wpool": wpool,
        "gvec": gvec, "fg": fg, "pscale": pscale, "rb": np.ascontiguousarray(np.asarray(rel_bias, f32)),
        "sel": _sel_const().reshape(3 * SELR, 128 * NSLOT),
        "invc": _invc_const(C.T).reshape(4 * 128, C.T),
        "ident": np.eye(128, dtype=f32),
    }


def run(cfg, seqs_per_core, weights, n_cores, trace=False):
    nc = build_program(cfg)
    shared = _layout_weights(cfg, **weights)
    in_maps = []
    for c in range(n_cores):
        xin = np.concatenate([np.ascontiguousarray(np.asarray(x, np.float32).T) for x in seqs_per_core[c]], axis=0)
        m = dict(shared)
        m["xin"] = xin
        in_maps.append(m)
    res = run_bass_kernel_spmd(nc, in_maps, core_ids=list(range(n_cores)), trace=trace)
    outs = []
    for c in range(n_cores):
        yT = res.results[c]["yT"].reshape(cfg.NSEQ, cfg.D, cfg.T)
        outs.append([np.ascontiguousarray(yT[s].T) for s in range(cfg.NSEQ)])
    return outs, res


def kernel(x_prompt, x_sample, norm_g, ffn_gate, ffn_up, ffn_down, w_in, w_pool, pool_scale,
           w_out, rel_bias, final_g):
    cfg = Cfg()
    x_prompt = np.asarray(x_prompt)
    x_sample = np.asarray(x_sample)
    seqs = [[x_prompt[2 * c], x_prompt[2 * c + 1], x_sample[c]] for c in range(8)]
    weights = dict(norm_g=norm_g, ffn_gate=ffn_gate, ffn_up=ffn_up, ffn_down=ffn_down, w_in=w_in,
                   w_pool=w_pool, pool_scale=pool_scale, w_out=w_out, rel_bias=rel_bias, final_g=final_g)
    outs, _ = run(cfg, seqs, weights, 8)
    y_prompt = np.empty(x_prompt.shape, np.float32)
    y_sample = np.empty(x_sample.shape, np.float32)
    for c in range(8):
        y_prompt[2 * c] = outs[c][0]
        y_prompt[2 * c + 1] = outs[c][1]
        y_sample[c] = outs[c][2]
    return (y_prompt, y_sample)
```
